# Optimizing a Trainium2 kernel written in Bass

```python
import jax, jax.numpy as jnp
from jax import lax
import numpy as np

D_MODEL = 1024
BATCH = 16
SEQ = 256
DEPTH = 2
DEC_BATCH = 8
DEC_SEQ = 4096
PAST_LEN = 512

GRID_W = 64
HEAD_DIM = 128
N_Q_HEADS = 8
N_KV_HEADS = 2
Q_PER_KV = N_Q_HEADS // N_KV_HEADS
ATTN_WIDTH = N_Q_HEADS * HEAD_DIM
KV_WIDTH = N_KV_HEADS * HEAD_DIM
Q_BLOCK = 128
ROPE_THETA = 10000.0
ROPE_PAIRS = HEAD_DIM // 4
POOL_WINDOWS = (2, 4, 8, 16)
N_POOL_GROUPS = 4
POOL_WIDTH = D_MODEL // 2
POOL_GROUP_DIM = POOL_WIDTH // N_POOL_GROUPS
CHUNK = 128
N_SGU_GROUPS = 4
SGU_WIDTH = D_MODEL // 2
SGU_GROUP_DIM = SGU_WIDTH // N_SGU_GROUPS
N_BRANCHES = 3
IN_WIDTH = ATTN_WIDTH + 2 * KV_WIDTH + POOL_WIDTH + 2 * SGU_WIDTH + N_BRANCHES * D_MODEL
SPLIT_Q = ATTN_WIDTH
SPLIT_K = SPLIT_Q + KV_WIDTH
SPLIT_V = SPLIT_K + KV_WIDTH
SPLIT_POOL = SPLIT_V + POOL_WIDTH
SPLIT_U = SPLIT_POOL + SGU_WIDTH
SPLIT_SV = SPLIT_U + SGU_WIDTH
D_FF = ((8 * D_MODEL + 3 * 256 - 1) // (3 * 256)) * 256
DEEPNORM_ALPHA = (2 * DEPTH) ** 0.25
DEEPNORM_BETA = (8 * DEPTH) ** -0.25
EPS = 1e-6

kernel_name = "hybrid_dit_gated_attn_pool_sgu_step"


def layer_norm(x, g, b):
    xf = x.astype(jnp.float32)
    mu = jnp.mean(xf, axis=-1, keepdims=True)
    xc = xf - mu
    var = jnp.mean(xc * xc, axis=-1, keepdims=True)
    y = xc * lax.rsqrt(var + EPS) * g.astype(jnp.float32) + b.astype(jnp.float32)
    return y.astype(x.dtype)


def plain_layer_norm(x):
    xf = x.astype(jnp.float32)
    mu = jnp.mean(xf, axis=-1, keepdims=True)
    xc = xf - mu
    var = jnp.mean(xc * xc, axis=-1, keepdims=True)
    return (xc * lax.rsqrt(var + EPS)).astype(x.dtype)


def rms_norm(x, g):
    xf = x.astype(jnp.float32)
    y = xf * lax.rsqrt(jnp.mean(xf * xf, axis=-1, keepdims=True) + EPS) * g.astype(jnp.float32)
    return y.astype(x.dtype)


def axial_rope_tables(rows):
    row = jnp.repeat(jnp.arange(rows), GRID_W).astype(jnp.float32)
    col = jnp.tile(jnp.arange(GRID_W), rows).astype(jnp.float32)
    inv_freq = ROPE_THETA ** (-jnp.arange(ROPE_PAIRS, dtype=jnp.float32) / ROPE_PAIRS)
    ang = jnp.stack([row[:, None] * inv_freq, col[:, None] * inv_freq], axis=1)
    return jnp.cos(ang), jnp.sin(ang)


def apply_axial_rope(x, cos, sin):
    B, S, H, _ = x.shape
    xr = x.astype(jnp.float32).reshape(B, S, H, 2, 2, ROPE_PAIRS)
    x1 = xr[..., 0, :]
    x2 = xr[..., 1, :]
    c = cos[None, :, None]
    s = sin[None, :, None]
    out = jnp.stack([x1 * c - x2 * s, x2 * c + x1 * s], axis=-2)
    return out.reshape(x.shape).astype(x.dtype)


def blocked_attention(q, k, v):
    B, S = q.shape[0], q.shape[1]
    nb = S // Q_BLOCK
    qb = q.reshape(B, nb, Q_BLOCK, N_KV_HEADS, Q_PER_KV, HEAD_DIM).transpose(1, 0, 2, 3, 4, 5)
    scale = HEAD_DIM ** -0.5

    def one_block(qblk):
        s = jnp.einsum('bqhgd,bkhd->bhgqk', qblk, k).astype(jnp.float32) * scale
        p = jax.nn.softmax(s, axis=-1)
        return jnp.einsum('bhgqk,bkhd->bqhgd', p.astype(v.dtype), v)

    out = lax.map(one_block, qb)
    return out.transpose(1, 0, 2, 3, 4, 5).reshape(B, S, ATTN_WIDTH)


def pool_mixer(xp, w_pg, pscale):
    B, S, _ = xp.shape
    xf = xp.astype(jnp.float32).reshape(B, S, N_POOL_GROUPS, POOL_GROUP_DIM)
    cs = jnp.concatenate([jnp.zeros((B, 1, N_POOL_GROUPS, POOL_GROUP_DIM), jnp.float32),
                          jnp.cumsum(xf, axis=1)], axis=1)
    w = jnp.array(POOL_WINDOWS, dtype=jnp.int32)[None, :]
    t = jnp.arange(S, dtype=jnp.int32)[:, None]
    lo = jnp.maximum(t - w // 2, 0)
    hi = jnp.minimum(t + (w - w // 2), S)
    g = jnp.arange(N_POOL_GROUPS, dtype=jnp.int32)[None, :]
    sums = cs[:, hi, g] - cs[:, lo, g]
    cnt = (hi - lo).astype(jnp.float32)[None, :, :, None]
    pooled = (sums / cnt - xf).astype(xp.dtype)
    y = jnp.einsum('bsgc,gcd->bsgd', pooled, w_pg).reshape(B, S, POOL_WIDTH)
    return y * pscale


def spatial_gating(u, v, w_s, b_s):
    B, S, _ = u.shape
    n = S // CHUNK
    vr = plain_layer_norm(v).reshape(B, n, CHUNK, N_SGU_GROUPS, SGU_GROUP_DIM)
    mixed = jnp.einsum('gpq,bnqgc->bnpgc', w_s, vr) + b_s.T[None, None, :, :, None]
    return u * mixed.reshape(B, S, SGU_WIDTH)


def trunk_layer(x, mod, rope, ctx_k, ctx_v, p):
    B, S, _ = x.shape
    sh1, sc1, g1, sh2, sc2, g2 = jnp.split(mod, 6, axis=-1)
    h = x * (1 + sc1) + sh1
    proj = h @ p['w_in']
    q, k, v, xp, xu, xv, gates = jnp.split(
        proj, [SPLIT_Q, SPLIT_K, SPLIT_V, SPLIT_POOL, SPLIT_U, SPLIT_SV], axis=-1)
    q = rms_norm(q.reshape(B, S, N_Q_HEADS, HEAD_DIM), p['q_norm_g'])
    k = rms_norm(k.reshape(B, S, N_KV_HEADS, HEAD_DIM), p['k_norm_g'])
    v = v.reshape(B, S, N_KV_HEADS, HEAD_DIM)
    if rope is None:
        k_all, v_all = k, v
    else:
        cos, sin = rope
        q = apply_axial_rope(q, cos, sin)
        k_lat = apply_axial_rope(k, cos, sin)
        k_all = jnp.concatenate([ctx_k, k_lat], axis=1)
        v_all = jnp.concatenate([ctx_v, v], axis=1)
    attn = blocked_attention(q, k_all, v_all)
    pool = pool_mixer(xp, p['w_pool_g'], p['pool_scale'])
    sgu = spatial_gating(jax.nn.gelu(xu, approximate=False), jax.nn.gelu(xv, approximate=False),
                         p['w_sgu'], p['b_sgu'])
    ga, gp, gs = jnp.split(jax.nn.sigmoid(gates), N_BRANCHES, axis=-1)
    merged = ga * (attn @ p['w_attn_o']) + gp * (pool @ p['w_pool_o']) + gs * (sgu @ p['w_sgu_o'])
    mix = merged @ p['w_out']
    x = layer_norm(DEEPNORM_ALPHA * x + g1 * mix, p['ln1_g'], p['ln1_b'])
    h2 = x * (1 + sc2) + sh2
    a, b = jnp.split(h2 @ p['w_ffn_in'], 2, axis=-1)
    f = (jax.nn.silu(a) * b) @ p['w_ffn_out']
    x = layer_norm(DEEPNORM_ALPHA * x + g2 * f, p['ln2_g'], p['ln2_b'])
    return x, k, v


def setup_inputs(seed: int = 0) -> dict:
    key = jax.random.key(seed)
    ks = jax.random.split(key, 25)
    nrm = jax.random.normal
    D = D_MODEL
    return {
        "x_prompt": nrm(ks[0], (BATCH, SEQ, D), jnp.float32),
        "x_sample": nrm(ks[1], (DEC_BATCH, DEC_SEQ, D), jnp.float32),
        "cache_k": nrm(ks[2], (DEC_BATCH, DEPTH, PAST_LEN, N_KV_HEADS, HEAD_DIM), jnp.float32),
        "cache_v": nrm(ks[3], (DEC_BATCH, DEPTH, PAST_LEN, N_KV_HEADS, HEAD_DIM), jnp.float32),
        "c": nrm(ks[4], (DEC_BATCH, D), jnp.float32),
        "c_ctx": nrm(ks[5], (D,), jnp.float32),
        "w_ada": nrm(ks[6], (DEPTH, D, 6 * D), jnp.float32) * (0.5 * D ** -0.5),
        "b_ada": nrm(ks[7], (DEPTH, 6 * D), jnp.float32) * 0.02,
        "w_in": nrm(ks[8], (DEPTH, D, IN_WIDTH), jnp.float32) * D ** -0.5,
        "q_norm_g": 1.0 + 0.02 * nrm(ks[9], (DEPTH, HEAD_DIM), jnp.float32),
        "k_norm_g": 1.0 + 0.02 * nrm(ks[10], (DEPTH, HEAD_DIM), jnp.float32),
        "w_pool_g": nrm(ks[11], (DEPTH, N_POOL_GROUPS, POOL_GROUP_DIM, POOL_GROUP_DIM), jnp.float32) * POOL_GROUP_DIM ** -0.5,
        "pool_scale": 1.0 + 0.1 * nrm(ks[12], (DEPTH, POOL_WIDTH), jnp.float32),
        "w_sgu": nrm(ks[13], (DEPTH, N_SGU_GROUPS, CHUNK, CHUNK), jnp.float32) * CHUNK ** -0.5,
        "b_sgu": 1.0 + 0.1 * nrm(ks[14], (DEPTH, N_SGU_GROUPS, CHUNK), jnp.float32),
        "w_attn_o": nrm(ks[15], (DEPTH, ATTN_WIDTH, D), jnp.float32) * (ATTN_WIDTH ** -0.5 * DEEPNORM_BETA),
        "w_pool_o": nrm(ks[16], (DEPTH, POOL_WIDTH, D), jnp.float32) * (POOL_WIDTH ** -0.5 * DEEPNORM_BETA),
        "w_sgu_o": nrm(ks[17], (DEPTH, SGU_WIDTH, D), jnp.float32) * (SGU_WIDTH ** -0.5 * DEEPNORM_BETA),
        "w_out": nrm(ks[18], (DEPTH, D, D), jnp.float32) * (D ** -0.5 * DEEPNORM_BETA),
        "ln1_g": 1.0 + 0.02 * nrm(ks[19], (DEPTH, D), jnp.float32),
        "ln1_b": 0.02 * nrm(ks[20], (DEPTH, D), jnp.float32),
        "w_ffn_in": nrm(ks[21], (DEPTH, D, 2 * D_FF), jnp.float32) * D ** -0.5,
        "w_ffn_out": nrm(ks[22], (DEPTH, D_FF, D), jnp.float32) * (D_FF ** -0.5 * DEEPNORM_BETA),
        "ln2_g": 1.0 + 0.02 * nrm(ks[23], (DEPTH, D), jnp.float32),
        "ln2_b": 0.02 * nrm(ks[24], (DEPTH, D), jnp.float32),
    }


def reference(x_prompt, x_sample, cache_k, cache_v, c, c_ctx, w_ada, b_ada, w_in, q_norm_g,
              k_norm_g, w_pool_g, pool_scale, w_sgu, b_sgu, w_attn_o, w_pool_o, w_sgu_o, w_out,
              ln1_g, ln1_b, w_ffn_in, w_ffn_out, ln2_g, ln2_b):
    rows = x_sample.shape[1] // GRID_W
    rope = axial_rope_tables(rows)
    silu_ctx = jax.nn.silu(c_ctx)
    silu_c = jax.nn.silu(c)
    y_p = x_prompt
    y_s = x_sample
    new_k = []
    new_v = []
    for l in range(DEPTH):
        p = {
            'w_in': w_in[l], 'q_norm_g': q_norm_g[l], 'k_norm_g': k_norm_g[l],
            'w_pool_g': w_pool_g[l], 'pool_scale': pool_scale[l],
            'w_sgu': w_sgu[l], 'b_sgu': b_sgu[l],
            'w_attn_o': w_attn_o[l], 'w_pool_o': w_pool_o[l], 'w_sgu_o': w_sgu_o[l],
            'w_out': w_out[l], 'ln1_g': ln1_g[l], 'ln1_b': ln1_b[l],
            'w_ffn_in': w_ffn_in[l], 'w_ffn_out': w_ffn_out[l],
            'ln2_g': ln2_g[l], 'ln2_b': ln2_b[l],
        }
        mod_ctx = (silu_ctx @ w_ada[l] + b_ada[l])[None, None, :]
        mod_lat = (silu_c @ w_ada[l] + b_ada[l])[:, None, :]
        y_p, k_ctx, v_ctx = trunk_layer(y_p, mod_ctx, None, None, None, p)
        new_k.append(k_ctx)
        new_v.append(v_ctx)
        y_s, _, _ = trunk_layer(y_s, mod_lat, rope, cache_k[:, l], cache_v[:, l], p)
    new_cache_k = jnp.stack(new_k, axis=1)
    new_cache_v = jnp.stack(new_v, axis=1)
    return (y_p, y_s, new_cache_k, new_cache_v)
```

```python
import numpy as np
from concourse.bass_utils import run_bass_kernel_spmd
from contextlib import ExitStack
import concourse.bass as bass
import concourse.mybir as mybir

F32 = mybir.dt.float32
BF16 = mybir.dt.bfloat16
ALU = mybir.AluOpType
AF = mybir.ActivationFunctionType
AX = mybir.AxisListType
_ESZ = {F32: 4, BF16: 2}
ENGS = ("pe", "act", "dve", "pool", "sp")
RING = 16
EPOCH = 1024


def _esz(dt):
    if dt in _ESZ:
        return _ESZ[dt]
    return mybir.dt.size(dt) if hasattr(mybir.dt, "size") else 4


class Sched:
    def __init__(self, nc):
        self.nc = nc
        self.q = {e: [] for e in ENGS}
        self.acc = {}
        self.dma_cnt = {e: 0 for e in ENGS}
        self.dmas = []
        self.out_dmas = []
        self.banks_free = list(range(8))
        self._slot_last = {}
        self.cells = {}
        self._csz = {}

    def box(self, ap):
        t = ap.tensor
        es = _esz(ap.dtype)
        dims = ap.ap
        sp = str(ap.space)
        if "DRAM" in sp.upper() or "HBM" in sp.upper():
            ext = sum((c - 1) * abs(s) for s, c in dims) + 1
            return ("DR:" + t.name, 0, 1, ap.offset * es, (ap.offset + ext) * es)
        if "PSUM" in sp.upper():
            return (t.name, 0, 128, 0, 2048)
        shp = list(t.shape)
        psz = 1
        for v in shp[1:]:
            psz *= v
        psz_b = psz * _esz(t.dtype)
        off_b = ap.offset * es
        p0 = off_b // psz_b
        f0 = off_b % psz_b
        npart = dims[0][1]
        ext = sum((c - 1) * abs(s) for s, c in dims[1:]) + 1
        return (t.name, p0, p0 + npart, f0, f0 + ext * es)

    @staticmethod
    def _ov(a, b):
        return a[1] < b[2] and b[1] < a[2] and a[3] < b[4] and b[3] < a[4]

    @staticmethod
    def _contains(a, b):
        return a[1] <= b[1] and a[2] >= b[2] and a[3] <= b[3] and a[4] >= b[4]

    def _access(self, ap, is_write, ref, deps, eng):
        bx = self.box(ap)
        name = bx[0]
        ent = self.acc.setdefault(name, {})
        cells = self.cells.setdefault(name, {})
        csz = 1024 if (bx[2] - bx[1] > 1 or bx[4] < (1 << 20)) and not name.startswith("DR:") else (1 << 18)
        if name not in self._csz:
            self._csz[name] = csz
        csz = self._csz[name]
        crange = range(bx[3] // csz, (bx[4] - 1) // csz + 1)
        cand = set()
        for c in crange:
            s = cells.get(c)
            if s:
                cand |= s
        for ob in cand:
            if not self._ov(bx, ob):
                continue
            rec = ent[ob]
            if rec[0] is not None:
                deps.append((rec[0], "waw" if is_write else "raw"))
            if is_write:
                for r in rec[1].values():
                    deps.append((r, "war"))
        if is_write:
            for ob in [ob for ob in cand if self._contains(bx, ob)]:
                del ent[ob]
                for c in range(ob[3] // csz, (ob[4] - 1) // csz + 1):
                    cells[c].discard(ob)
            ent[bx] = [ref, {}]
            for c in crange:
                cells.setdefault(c, set()).add(bx)
        else:
            rec = ent.get(bx)
            if rec is None:
                rec = ent[bx] = [None, {}]
                for c in crange:
                    cells.setdefault(c, set()).add(bx)
            rec[1][eng if ref[0] == "e" else ref] = ref

    def op(self, eng, fn, reads=(), writes=(), track=True):
        idx = len(self.q[eng])
        ref = ("e", eng, idx)
        deps = []
        for ap in reads:
            if ap is not None and not isinstance(ap, (int, float)):
                self._access(ap, False, ref, deps, eng)
        for ap in writes:
            if ap is not None:
                self._access(ap, True, ref, deps, eng)
        self.q[eng].append(dict(fn=fn, deps=deps, inc=False, kind="op", tag=getattr(self, "tag", "?")))
        return ref

    def dma(self, out, in_, queue="sp", track_in=True, track_out=True, is_output=False, **kw):
        n = self.dma_cnt[queue]
        self.dma_cnt[queue] = n + 1
        slot = n % RING
        cum = 16 * (n // RING + 1)
        did = len(self.dmas)
        self.dmas.append((queue, slot, cum))
        ref = ("d", did)
        deps = []
        if n >= RING:
            prev = self._slot_last[(queue, slot)]
            deps.append((("d", prev), "ring"))
        self._slot_last[(queue, slot)] = did
        if track_in:
            self._access(in_, False, ref, deps, queue)
        if track_out:
            self._access(out, True, ref, deps, queue)
        if is_output:
            self.out_dmas.append(did)
        self.q[queue].append(dict(fn=None, deps=deps, inc=False, kind="dma", out=out, in_=in_, did=did, kw=kw))
        return ref

    def barrier(self):
        last = {e: len(self.q[e]) - 1 for e in ENGS}
        ndma = len(self.dmas)
        for e in ENGS:
            deps = []
            for e2 in ENGS:
                if e2 != e and last[e2] >= 0:
                    k = last[e2]
                    while k >= 0 and self.q[e2][k]["kind"] != "op":
                        k -= 1
                    if k >= 0:
                        deps.append((("e", e2, k), "raw"))
            seenq = {}
            for d in range(ndma - 1, -1, -1):
                qn = self.dmas[d][0]
                if seenq.get(qn, 0) < RING:
                    seenq[qn] = seenq.get(qn, 0) + 1
                    deps.append((("d", d), "raw"))
            self.q[e].append(dict(fn=None, deps=deps, inc=False, kind="nop"))

    def bank(self):
        return self.banks_free.pop(0)

    def free(self, b):
        self.banks_free.append(b)

    def emit(self, es):
        nc = self.nc
        for e in ENGS:
            for op in self.q[e]:
                keep = []
                best = {}
                for ref, kind in op["deps"]:
                    if ref[0] == "e":
                        if ref[1] == e and (e == "pe" or kind != "raw"):
                            continue
                        if ref[2] > best.get(ref[1], -1):
                            best[ref[1]] = ref[2]
                    else:
                        keep.append(ref)
                for e2, k in best.items():
                    self.q[e2][k]["inc"] = True
                    keep.append(("e", e2, k))
                op["deps"] = keep
        cnt = {}
        for e in ENGS:
            c = 0
            for i, op in enumerate(self.q[e]):
                if op["inc"]:
                    c += 1
                cnt[(e, i)] = c
        tot = {e: (cnt[(e, len(self.q[e]) - 1)] if self.q[e] else 0) for e in ENGS}
        sem_ep = {e: [es.enter_context(nc.semaphore("s_%s_%d" % (e, k))) for k in range(max(1, (tot[e] + EPOCH - 1) // EPOCH))] for e in ENGS}
        dq = [e for e in ENGS if self.dma_cnt[e] > 0]
        sem_d = {(e, s): es.enter_context(nc.semaphore("d_%s_%d" % (e, s))) for e in dq for s in range(min(RING, self.dma_cnt[e]))}
        handles = {"pe": nc.tensor, "act": nc.scalar, "dve": nc.vector, "pool": nc.gpsimd, "sp": nc.sync}
        stats = {e: [0, 0, 0] for e in ENGS}

        def emit_eng(e, h):
            seen_e = {}
            seen_d = {}
            for i, op in enumerate(self.q[e]):
                need_e = {}
                need_d = {}
                for ref in op["deps"]:
                    if ref[0] == "e":
                        v = cnt[(ref[1], ref[2])]
                        if v > seen_e.get(ref[1], 0) and v > need_e.get(ref[1], 0):
                            need_e[ref[1]] = v
                    else:
                        qn, slot, cum = self.dmas[ref[1]]
                        if cum > seen_d.get((qn, slot), 0) and cum > need_d.get((qn, slot), 0):
                            need_d[(qn, slot)] = cum
                waits = [(sem_ep[k][(v - 1) // EPOCH], (v - 1) % EPOCH + 1) for k, v in need_e.items()]
                waits += [(sem_d[k], v) for k, v in need_d.items()]
                for k, v in need_e.items():
                    seen_e[k] = v
                for k, v in need_d.items():
                    seen_d[k] = v
                emb = None
                if op["kind"] == "op" and waits:
                    emb = waits.pop()
                for sm, v in waits:
                    h.wait_ge(sm, v)
                    stats[e][1] += 1
                if op["kind"] == "op":
                    ins = op["fn"](h)
                    if emb is not None:
                        ins._wait_ge(emb[0], emb[1])
                    stats[e][0] += 1
                    if op["inc"]:
                        ins.then_inc(sem_ep[e][(cnt[(e, i)] - 1) // EPOCH], 1)
                        stats[e][2] += 1
                elif op["kind"] == "dma":
                    qn, slot, cum = self.dmas[op["did"]]
                    h.dma_start(out=op["out"], in_=op["in_"], **op["kw"]).then_inc(sem_d[(qn, slot)], 16)
                    stats[e][0] += 1
            if e == "sp":
                fin = {}
                for d in self.out_dmas:
                    qn, slot, cum = self.dmas[d]
                    fin[(qn, slot)] = max(fin.get((qn, slot), 0), cum)
                for k, v in fin.items():
                    if v > seen_d.get(k, 0):
                        h.wait_ge(sem_d[k], v)

        with nc.Block() as block:
            @block.tensor
            def _(h):
                emit_eng("pe", h)

            @block.scalar
            def _(h):
                emit_eng("act", h)

            @block.vector
            def _(h):
                emit_eng("dve", h)

            @block.gpsimd
            def _(h):
                emit_eng("pool", h)

            @block.sync
            def _(h):
                emit_eng("sp", h)
        self.stats = stats

    def mm(self, out, lhsT, rhs, start=True, stop=True):
        return self.op("pe", lambda h: h.matmul(out, lhsT, rhs, start=start, stop=stop), [lhsT, rhs], [out])

    def tr(self, out, in_, ident):
        return self.op("pe", lambda h: h.transpose(out, in_, ident), [in_, ident], [out])

    def act(self, out, in_, func, bias=None, scale=None, accum_out=None, eng="act"):
        kw = {}
        if bias is not None:
            kw["bias"] = bias
        if scale is not None:
            kw["scale"] = scale
        if accum_out is not None:
            kw["accum_out"] = accum_out
        rd = [in_] + [a for a in (bias, scale) if a is not None and not isinstance(a, (int, float))]
        wr = [out] + ([accum_out] if accum_out is not None else [])
        return self.op("act", lambda h: h.activation(out, in_, func, **kw), rd, wr)

    def tt(self, eng, out, in0, in1, op):
        return self.op(eng, lambda h: h.tensor_tensor(out, in0, in1, op), [in0, in1], [out])

    def ts(self, eng, out, in0, s1, s2, op0, op1=None):
        rd = [in0] + [a for a in (s1, s2) if a is not None and not isinstance(a, (int, float))]
        if op1 is None:
            name = {ALU.add: "tensor_scalar_add", ALU.mult: "tensor_scalar_mul", ALU.subtract: "tensor_scalar_sub"}[op0]
            return self.op(eng, lambda h: getattr(h, name)(out, in0, s1), rd, [out])
        return self.op(eng, lambda h: h.tensor_scalar(out, in0, s1, s2, op0, op1), rd, [out])

    def stt(self, eng, out, in0, scalar, in1, op0, op1):
        rd = [in0, in1] + ([scalar] if not isinstance(scalar, (int, float)) else [])
        return self.op(eng, lambda h: h.scalar_tensor_tensor(out, in0, scalar, in1, op0, op1), rd, [out])

    def copy(self, eng, out, in_):
        if eng == "act":
            return self.op("act", lambda h: h.copy(out, in_), [in_], [out])
        return self.op(eng, lambda h: h.tensor_copy(out, in_), [in_], [out])

    def memset(self, eng, ap, val):
        return self.op(eng, lambda h: h.memset(ap, val), [], [ap])

D = 1024
DEPTH = 2
ALPHA = float((2 * DEPTH) ** 0.25)
EPS = 1e-6
NSAMP_T = 8
KOFF_P = 4608
ATT_SCALE = float(128 ** -0.5)
NUNITS = 33
UNIT = 5120
U_KV, U_PIN, U_Q, U_XU, U_XV, U_MRG, U_WOUT, U_FIN, U_FOUT = 0, 1, 2, 4, 5, 6, 14, 16, 27
POOLW = (2, 4, 8, 16)


def bcast(ap, pre=(), post=()):
    dims = [list(ap.ap[0])] + [[0, n] for n in pre] + [list(d) for d in ap.ap[1:]] + [[0, n] for n in post]
    return bass.AP(ap.tensor, ap.offset, dims)


def build_program(phases=("prep", "l0", "l1"), dbg_l0_out=False, dbg=None):
    nc = bass.Bass("TRN2", target_bir_lowering=False)

    def din(name, shape, dt=F32):
        return nc.dram_tensor(name, list(shape), dt, kind="ExternalInput").ap()

    def dout(name, shape):
        return nc.dram_tensor(name, list(shape), F32, kind="ExternalOutput").ap()

    def dscr(name, shape, dt=F32):
        return nc.dram_tensor(name, list(shape), dt, kind="Internal").ap()

    xs_d = din("xs", [4096, D]); xp_d = din("xp", [512, D])
    ck_d = din("ck", [2, 512, 256]); cv_d = din("cv", [2, 512, 256])
    cT_d = din("cT", [128, 8, 2])
    wada_d = din("w_ada", [2, D, 6 * D]); badaT_d = din("b_adaT", [128, 2, 48])
    win_d = din("w_in", [2, D, 6144]); wao_d = din("w_attn_o", [2, D, D])
    wpo_d = din("w_pool_o", [2, 512, D]); wso_d = din("w_sgu_o", [2, 512, D])
    wout_d = din("w_out", [2, D, D]); wfi_d = din("w_ffn_in", [2, D, 5632]); wfo_d = din("w_ffn_out", [2, 2816, D])
    wpg_d = din("w_pg", [2, 128, 4, 128]); wsT_d = din("w_sT", [2, 128, 4, 128])
    bsgu_d = din("bsgu", [2, 128, 4, 128]); pscT_d = din("pscT", [128, 2, 4])
    qkg_d = din("qkg", [2, 128, 2, 128]); lnbc_d = din("lnbc", [2, 4, 128, D]); lnT_d = din("lnT", [128, 2, 2, 8])
    ropeR_d = din("ropeR", [128, 2, 32, 32]); ropeC_d = din("ropeC", [128, 2, 32]); idf_d = din("idf", [128, 128])
    ys_d = dout("ys", [4096, D]); yp_d = dout("yp", [512, D])
    nk_d = dout("nk", [2, 2, 256, 256]); nv_d = dout("nv", [2, 2, 256, 256])
    x1s_d = dscr("x1s", [4096, D]); x1p_d = dscr("x1p", [512, D])
    xps_d = dscr("xps", [4, 128, 4112]); xpp_d = dscr("xpp", [4, 128, 2, 272])
    wsc_d = [dscr("wsc%d" % l, [NUNITS, 128, UNIT], BF16) for l in range(2)]

    S = Sched(nc)
    es = ExitStack()
    with es:
        AW = 52736
        arena = es.enter_context(nc.sbuf_tensor("arena", [128, AW], F32))
        ps = [es.enter_context(nc.psum_tensor("ps%d" % i, [128, 512], F32)) for i in range(8)]
        psb = [p[:].bitcast(BF16) for p in ps]

        def carve(off_b, shape, dt):
            n = 1
            for v in shape[1:]:
                n *= v
            nb = n * (4 if dt == F32 else 2)
            assert off_b % 4 == 0 and off_b + nb <= AW * 4, (off_b, nb)
            a = arena[:, off_b // 4:(off_b + nb + 3) // 4]
            if dt != F32:
                a = a.bitcast(dt)
            if len(shape) == 3:
                a = a.rearrange("p (a b) -> p a b", b=shape[2])
            elif len(shape) == 4:
                a = a.rearrange("p (a b c) -> p a b c", b=shape[2], c=shape[3])
            return a

        cur = [0]

        def palloc(shape, dt):
            n = 1
            for v in shape[1:]:
                n *= v
            nb = (n * (4 if dt == F32 else 2) + 3) // 4 * 4
            a = carve(cur[0], shape, dt)
            cur[0] += nb
            return a

        KT = palloc([128, 2, 5120], BF16)
        Vst = palloc([128, 40, 256], BF16)
        ident_f = palloc([128, 128], F32); ident_b = palloc([128, 128], BF16)
        ones_f = palloc([128, 128], F32); ones_b = palloc([128, 128], BF16)
        ropeR = palloc([128, 2, 32, 32], F32); ropeC = palloc([128, 2, 32], F32)
        qkg = palloc([128, 2, 128], F32); bsgu = palloc([128, 4, 128], F32)
        pscT = palloc([128, 2, 4], F32); lnT = palloc([128, 2, 2, 8], F32)
        badaT = palloc([128, 2, 48], F32); sil = palloc([128, 8, 2], F32)
        mod = palloc([128, 48, 2], F32); scal = palloc([128, 2, 4, 8], F32)
        wpgb = palloc([128, 4, 128], BF16); wsTb = palloc([128, 4, 128], BF16)
        small = palloc([128, 64], F32)
        small2 = palloc([128, 64], F32)
        zero_t = palloc([128, 16], F32)
        eps_t = palloc([128, 16], F32)
        gbc = [[palloc([128, D], F32) for _ in range(2)] for _ in range(2)]
        bcr = [palloc([128, D], F32) for _ in range(2)]
        span0 = cur[0]
        wring = [palloc([128, UNIT], BF16) for _ in range(3)]
        hT = palloc([128, 8, 512], BF16)
        xt = palloc([128, 4, D], F32)
        P0 = cur[0]
        PA = 38912
        assert P0 + PA <= AW * 4, (P0, PA, AW * 4)
        R = P0 + 16384
        attnT = carve(P0, [128, 8, 512], BF16)
        sguT = carve(P0 + 8192, [128, 4, 512], BF16)
        poolT = carve(P0 + 12288, [128, 4, 512], BF16)
        qr = carve(R, [128, 4, D], BF16); qT = carve(R + 8192, [128, 8, 512], BF16)
        nsq = carve(R + 16384, [128, 512], F32); nqn = carve(R + 18432, [128, 512], F32); nB = carve(R + 20480, [128, 512], F32)
        N0 = P0 + PA
        X = N0 + 6144
        assert X + 22528 <= AW * 4, (X, AW * 4)
        Pr = [carve(N0 + 1024 * i, [128, 512], BF16) for i in range(4)]
        rden = carve(N0 + 4096, [128, 512], F32)
        tsetB = [carve(N0 + 2048 * i, [128, 512], F32) for i in range(3)]
        xuT = carve(X, [128, 4, 512], F32); vr = carve(X + 8192, [128, 4, 512], BF16)
        xvt4 = [carve(X + 12288 + 2048 * i, [128, 512], F32) for i in range(4)]
        stmp = carve(X + 20480, [128, 512], F32)
        xnext = carve(X, [128, 4, D], F32)
        xpb = carve(X, [128, 4, 544], F32)
        sA = carve(X + 8704, [128, 544], F32); sB = carve(X + 8704 + 2176, [128, 544], F32); sC = carve(X + 8704 + 4352, [128, 544], F32)
        pooledT = carve(X + 15232, [128, 4, 512], BF16)
        sig = [carve(R + 2048 * i, [128, 512], F32) for i in range(3)]
        mtm = carve(R + 6144, [128, 512], F32); mtt = carve(R + 8192, [128, 512], F32)
        mergedT = carve(R + 10240, [128, 8, 512], BF16)
        wtmp = carve(R, [128, 512], F32)
        uT = carve(P0, [128, 22, 512], BF16)
        ftmp = carve(P0 + 22528, [128, 512], F32)
        kr = carve(R, [128, 4, 256], BF16); vtmp = carve(R + 2048, [128, 256], F32)
        xpstage = carve(R + 4096, [128, 4, 512], F32)
        cstage = carve(R, [128, 4, 256], F32); cstb = carve(R + 4096, [128, 4, 256], BF16)
        st32 = [carve(span0 + 20480 * i, [128, UNIT], F32) for i in range(3)]
        st16 = [carve(span0 + 61440 + 10240 * i, [128, UNIT], BF16) for i in range(3)]
        ss = small[:, 0:8]; bst = small[:, 8:20]; mv = small[:, 20:22]; rstd = small[:, 22:23]; nmr = small[:, 23:24]
        ss2 = [small[:, 0:8], small[:, 24:32]]
        bst4 = small2[:, 0:48].rearrange("p (a b) -> p a b", b=12); mv4 = small2[:, 48:56].rearrange("p (a b) -> p a b", b=2)
        rstd4 = small2[:, 56:60]; nmr4 = small2[:, 60:64]

        S.dma(ident_f, idf_d)
        S.dma(ropeR, ropeR_d); S.dma(ropeC, ropeC_d)
        S.dma(pscT, pscT_d); S.dma(lnT, lnT_d); S.dma(badaT, badaT_d)
        S.dma(sil, cT_d)
        S.copy("dve", ident_b, ident_f)
        S.memset("dve", ones_f, 1.0); S.memset("dve", ones_b, 1.0)
        S.memset("dve", zero_t, 0.0)
        S.memset("dve", eps_t, EPS)
        S.act(sil, sil, AF.Silu)
        for g in range(4):
            S.dma(xps_d[g, :, 0:8], zero_t[:, 0:8]); S.dma(xps_d[g, :, 4104:4112], zero_t[:, 0:8])
            for sq_ in range(2):
                S.dma(xpp_d[g, :, sq_, 0:8], zero_t[:, 0:8]); S.dma(xpp_d[g, :, sq_, 264:272], zero_t[:, 0:8])

        def wsrc(w, l, k0, k1, n0, n1):
            return w[l].rearrange("(c p) n -> p c n", p=128)[:, k0:k1, n0:n1]

        def unit_pieces(l, u):
            if u == U_KV:
                return [(wsrc(win_d, l, 0, 8, 1024, 1536), (0, 8, 0, 512))], 8, 512
            if u == U_PIN:
                return [(wsrc(win_d, l, 0, 8, 1536, 2048), (0, 8, 0, 512))], 8, 512
            if u in (U_Q, U_Q + 1):
                h = u - U_Q
                return [(wsrc(win_d, l, 0, 8, h * 512, h * 512 + 512), (0, 8, 0, 512))], 8, 512
            if u == U_XU:
                return [(wsrc(win_d, l, 0, 8, 2048, 2560), (0, 8, 0, 512))], 8, 512
            if u == U_XV:
                return [(wsrc(win_d, l, 0, 8, 2560, 3072), (0, 8, 0, 512))], 8, 512
            if U_MRG <= u < U_MRG + 8:
                j = u - U_MRG
                c0, c1 = j * 128, j * 128 + 128
                return [(wsrc(win_d, l, 0, 8, 3072 + c0, 3072 + c1), (0, 8, 0, 128)),
                        (wsrc(win_d, l, 0, 8, 4096 + c0, 4096 + c1), (8, 16, 0, 128)),
                        (wsrc(win_d, l, 0, 8, 5120 + c0, 5120 + c1), (16, 24, 0, 128)),
                        (wsrc(wao_d, l, 0, 8, c0, c1), (24, 32, 0, 128)),
                        (wsrc(wpo_d, l, 0, 4, c0, c1), (32, 36, 0, 128)),
                        (wsrc(wso_d, l, 0, 4, c0, c1), (36, 40, 0, 128))], 40, 128
            if u in (U_WOUT, U_WOUT + 1):
                h = u - U_WOUT
                return [(wsrc(wout_d, l, 0, 8, h * 512, h * 512 + 512), (0, 8, 0, 512))], 8, 512
            if U_FIN <= u < U_FIN + 11:
                m = u - U_FIN
                return [(wsrc(wfi_d, l, 0, 8, m * 256, m * 256 + 256), (0, 8, 0, 256)),
                        (wsrc(wfi_d, l, 0, 8, 2816 + m * 256, 2816 + m * 256 + 256), (0, 8, 256, 512))], 8, 512
            r = u - U_FOUT
            k0 = 4 * r
            k1 = min(22, k0 + 4)
            return [(wsrc(wfo_d, l, k0, k1, 0, 1024), (0, k1 - k0, 0, 1024))], k1 - k0, 1024

        def prep_all(layers):
            jobs = [(l, u) for l in layers for u in range(NUNITS)]

            def issue_in(k):
                l, u = jobs[k]
                pieces, nk, ncols = unit_pieces(l, u)
                v32 = st32[k % 3][:, 0:nk * ncols].rearrange("p (a b) -> p a b", b=ncols)
                for src, (k0, k1, c0, c1) in pieces:
                    S.dma(v32[:, k0:k1, c0:c1], src, track_in=False)

            for k in range(min(2, len(jobs))):
                issue_in(k)
            for k, (l, u) in enumerate(jobs):
                if k + 2 < len(jobs):
                    issue_in(k + 2)
                pieces, nk, ncols = unit_pieces(l, u)
                n = nk * ncols
                S.copy(("dve", "act")[k % 2], st16[k % 3][:, 0:n], st32[k % 3][:, 0:n])
                S.dma(wsc_d[l][u, :, 0:n], st16[k % 3][:, 0:n], queue="act")

        TILE_SEQ = ([(U_Q, 4096), (U_Q + 1, 4096), (U_XU, 4096), (U_XV, 4096)] + [(U_MRG + j, 5120) for j in range(8)]
                    + [(U_WOUT, 4096), (U_WOUT + 1, 4096)] + [(U_FIN + m, 4096) for m in range(11)]
                    + [(U_FOUT + r, 4096 if r < 5 else 2048) for r in range(6)])
        wstate = {"plan": [], "cur": 0, "issued": 0, "base": 0, "l": 0}

        def wplan_start(l, ntiles, base):
            wstate.update(plan=TILE_SEQ * ntiles, cur=0, issued=0, base=base, l=l)

        def wadvance(ahead=2):
            idx = wstate["cur"] - 1
            while wstate["issued"] < min(len(wstate["plan"]), idx + 1 + ahead):
                k = wstate["issued"]
                uu, nn = wstate["plan"][k]
                S.dma(wring[(wstate["base"] + k) % 3][:, 0:nn], wsc_d[wstate["l"]][uu, :, 0:nn])
                wstate["issued"] = k + 1

        def wload(l, u, n, ahead=2):
            idx = wstate["cur"]
            assert wstate["plan"][idx] == (u, n), (idx, u, n, wstate["plan"][idx])
            wstate["cur"] = idx + 1
            while wstate["issued"] < min(len(wstate["plan"]), idx + 1 + ahead):
                k = wstate["issued"]
                uu, nn = wstate["plan"][k]
                S.dma(wring[(wstate["base"] + k) % 3][:, 0:nn], wsc_d[wstate["l"]][uu, :, 0:nn])
                wstate["issued"] = k + 1
            return wring[(wstate["base"] + idx) % 3]

        def layer_setup(l, parts="ABCD"):
            S.tag = "setup"
            S.dma(qkg, qkg_d[l]); S.dma(bsgu, bsgu_d[l])
            S.dma(st32[0][:, 0:512].rearrange("p (a b) -> p a b", b=128), wpg_d[l], track_in=False)
            S.copy("dve", wpgb, st32[0][:, 0:512].rearrange("p (a b) -> p a b", b=128))
            S.dma(st32[0][:, 512:1024].rearrange("p (a b) -> p a b", b=128), wsT_d[l], track_in=False)
            S.copy("dve", wsTb, st32[0][:, 512:1024].rearrange("p (a b) -> p a b", b=128))
            if 'B' not in parts:
                return
            b = S.bank()
            nxt = None
            for uu in range(24):
                wa = wring[uu % 3][:, 0:UNIT].bitcast(F32)[:, 0:2048].rearrange("p (a b) -> p a b", b=256)
                S.dma(wa, wsrc(wada_d, l, 0, 8, uu * 256, uu * 256 + 256), track_in=False)
                for jj in range(2):
                    j = uu * 2 + jj
                    for kc in range(8):
                        S.mm(ps[b][:, 2 * j:2 * j + 2], wa[:, kc, jj * 128:(jj + 1) * 128], sil[:, kc, :], start=(kc == 0), stop=(kc == 7))
            S.tt("dve", mod, ps[b][:, 0:96].rearrange("p (a b) -> p a b", b=2), bcast(badaT[:, l, :], post=(2,)), ALU.add)
            S.free(b)
            if 'C' not in parts:
                return
            for g in range(2):
                S.ts("dve", scal[:, g, 0, :], mod[:, 8:16, g], 1.0, None, ALU.add)
                S.copy("dve", scal[:, g, 1, :], mod[:, 0:8, g])
                S.ts("dve", scal[:, g, 3, :], mod[:, 32:40, g], 1.0, None, ALU.add)
                S.tt("dve", scal[:, g, 2, :], lnT[:, l, 0, :], scal[:, g, 3, :], ALU.mult)
                S.tt("dve", scal[:, g, 3, :], lnT[:, l, 1, :], scal[:, g, 3, :], ALU.mult)
                S.tt("dve", scal[:, g, 3, :], scal[:, g, 3, :], mod[:, 24:32, g], ALU.add)
                if 'D' not in parts:
                    continue
                for gi, base in ((0, 16), (1, 40)):
                    for half in range(2):
                        b = S.bank()
                        for c4 in range(4):
                            c = half * 4 + c4
                            dg = (nsq, nqn)[c4 % 2][:, 0:128]
                            S.ts("dve", dg, ident_f, mod[:, base + c, g:g + 1], None, ALU.mult)
                            S.tag = "setup3"
                            S.mm(ps[b][:, c4 * 128:(c4 + 1) * 128], ones_f, dg)
                            S.tag = "setup"
                        S.copy("act", gbc[g][gi][:, half * 512:(half + 1) * 512], ps[b][:])
                        S.free(b)

        def load_x(src_d, row0, dst=None):
            dst = xt if dst is None else dst
            for s in range(4):
                S.dma(dst[:, s, :], src_d[row0 + s * 128: row0 + (s + 1) * 128, :], queue="pool")

        def make_hT(g, i_scale, i_shift, xsrc=None):
            xsrc = xt if xsrc is None else xsrc
            for c in range(8):
                b = S.bank()
                for s in range(4):
                    S.tr(ps[b][:, s * 128:(s + 1) * 128], xsrc[:, s, c * 128:(c + 1) * 128], ident_f)
                S.ts("dve", hT[:, c, :], ps[b][:], scal[:, g, i_scale, c:c + 1], scal[:, g, i_shift, c:c + 1], ALU.mult, ALU.add)
                S.free(b)

        def rms_rope(psv, nh, gain, st, out_bf, out_f32=None, tset=0):
            n = nh * 128
            t3 = (nsq, nqn, nB) if tset == 0 else tsetB
            sq = t3[0][:, 0:n]; qn = t3[1][:, 0:n]; B = t3[2][:, 0:n]
            ss = ss2[tset]
            S.act(sq, psv, AF.Square)
            S.op("dve", lambda h: h.reduce_sum(ss[:, 0:nh], sq.rearrange("p (a b) -> p a b", b=128), AX.X), [sq], [ss[:, 0:nh]])
            S.act(ss[:, 0:nh], ss[:, 0:nh], AF.Ln, bias=eps_t[:, 0:1], scale=1.0 / 128)
            S.act(ss[:, 0:nh], ss[:, 0:nh], AF.Exp, scale=-0.5)
            qn3 = qn.rearrange("p (a b) -> p a b", b=128)
            S.tt("dve", qn3, psv.rearrange("p (a b) -> p a b", b=128), bcast(ss[:, 0:nh], post=(128,)), ALU.mult)
            S.tt("dve", qn3, qn3, bcast(gain, pre=(nh,)), ALU.mult)
            if out_f32 is not None:
                S.dma(out_f32, qn, queue="pool", is_output=True, track_out=False)
            if st is None:
                S.copy("dve", out_bf, qn)
                return
            A3 = sq.rearrange("p (a b) -> p a b", b=128)
            B3 = B.rearrange("p (a b) -> p a b", b=128)
            for a in range(2):
                cs = ropeR[:, 0, st, :] if a == 0 else ropeC[:, 0, :]
                sn = ropeR[:, 1, st, :] if a == 0 else ropeC[:, 1, :]
                qa = qn3[:, :, a * 64:(a + 1) * 64].rearrange("p h (x i) -> p h x i", x=2)
                Aa = A3[:, :, a * 64:(a + 1) * 64].rearrange("p h (x i) -> p h x i", x=2)
                Ba = B3[:, :, a * 64:(a + 1) * 64].rearrange("p h (x i) -> p h x i", x=2)
                S.tt("pool", Aa, qa, bcast(cs, pre=(nh, 2)), ALU.mult)
                S.stt("dve", Ba[:, :, 0, :], qa[:, :, 1, :], -1.0, bcast(sn, pre=(nh,)), ALU.mult, ALU.mult)
                S.tt("dve", Ba[:, :, 1, :], qa[:, :, 0, :], bcast(sn, pre=(nh,)), ALU.mult)
            S.tt("dve", out_bf, sq, B, ALU.add)

        def prepass_tile(l, g, src_d, row0, key0, vch0, st0, wkv, wpin, xp_dst, nk_out=None, nv_out=None, xbuf=None, nxt=None):
            S.tag = "prepass"
            if nxt is not None:
                load_x(nxt[0], nxt[1], dst=nxt[2])
            make_hT(g, 0, 1, xsrc=xbuf)
            for s in range(4):
                b = S.bank(); b2 = S.bank()
                for kc in range(8):
                    S.mm(ps[b][:, 0:256], hT[:, kc, s * 128:(s + 1) * 128], wkv[:, kc, 0:256], start=(kc == 0), stop=(kc == 7))
                for kc in range(8):
                    S.mm(ps[b2][:, 0:256], hT[:, kc, s * 128:(s + 1) * 128], wkv[:, kc, 256:512], start=(kc == 0), stop=(kc == 7))
                if g == 0:
                    S.copy("act", Vst[:, vch0 + s, :], ps[b2][:, 0:256])
                    rms_rope(ps[b][:, 0:256], 2, qkg[:, 1, :], st0 + s, kr[:, s, :], tset=s % 2)
                else:
                    S.copy("act", vtmp, ps[b2][:, 0:256])
                    S.dma(nv_out(s), vtmp, queue="pool", is_output=True, track_out=False)
                    S.copy("dve", Vst[:, vch0 + s, :], vtmp)
                    rms_rope(ps[b][:, 0:256], 2, qkg[:, 1, :], None, kr[:, s, :], out_f32=nk_out(s), tset=s % 2)
                S.free(b); S.free(b2)
            for gg in range(4):
                b = S.bank()
                for kc in range(8):
                    S.mm(ps[b][:], wpin[:, kc, gg * 128:(gg + 1) * 128], hT[:, kc, :], start=(kc == 0), stop=(kc == 7))
                S.copy("act", xpstage[:, gg, :], ps[b][:])
                S.free(b)
                for dst, c0, c1 in xp_dst(gg):
                    S.dma(dst, xpstage[:, gg, c0:c1], queue="pool")
            for kvh in range(2):
                b = S.bank()
                for s in range(4):
                    S.tr(psb[b][:, s * 128:(s + 1) * 128], kr[:, s, kvh * 128:(kvh + 1) * 128], ident_b)
                S.copy("act", KT[:, kvh, key0:key0 + 512], psb[b][:, 0:512])
                S.free(b)

        def load_cache(l):
            S.tag = "cache"
            for (src, isk) in ((ck_d, True), (cv_d, False)):
                S.dma(cstage, src[l].rearrange("(c p) n -> p c n", p=128), queue="pool")
                if isk:
                    S.copy("dve", cstb, cstage)
                    for kvh in range(2):
                        b = S.bank()
                        for s in range(4):
                            S.tr(psb[b][:, s * 128:(s + 1) * 128], cstb[:, s, kvh * 128:(kvh + 1) * 128], ident_b)
                        S.copy("act", KT[:, kvh, 0:512], psb[b][:, 0:512])
                        S.free(b)
                else:
                    S.copy("dve", Vst[:, 0:4, :], cstage)

        def attention(h, kvh, qc0, nq, chunks):
            ob = S.bank(); db = S.bank()
            n = len(chunks)
            LA = 2
            sb = {}
            for i in range(n + LA):
                if i < n:
                    k0, vc = chunks[i]
                    b = S.bank(); sb[i] = b
                    S.mm(ps[b][:, 0:nq], KT[:, kvh, k0:k0 + 128], qT[:, h, qc0:qc0 + nq])
                    S.act(Pr[i % 4][:, 0:nq], ps[b][:, 0:nq], AF.Exp, scale=ATT_SCALE)
                    S.free(b)
                j = i - LA
                if j >= 0:
                    k0, vc = chunks[j]
                    S.mm(ps[ob][:, 0:nq], Vst[:, vc, kvh * 128:(kvh + 1) * 128], Pr[j % 4][:, 0:nq], start=(j == 0), stop=(j == n - 1))
                    S.mm(ps[db][:, 0:nq], ones_b, Pr[j % 4][:, 0:nq], start=(j == 0), stop=(j == n - 1))
            S.op("dve", lambda hh: hh.reciprocal(rden[:, 0:nq], ps[db][:, 0:nq]), [ps[db][:, 0:nq]], [rden[:, 0:nq]])
            S.tt("dve", attnT[:, h, qc0:qc0 + nq], ps[ob][:, 0:nq], rden[:, 0:nq], ALU.mult)
            S.free(ob); S.free(db)

        def attention2(h, kvh, qc0, nq, chunks, ob, db, srot):
            n = len(chunks)
            LA = 2
            for i in range(n + LA):
                if i < n:
                    k0, vc = chunks[i]
                    b = srot[i % 3]
                    S.mm(ps[b][:, 0:nq], KT[:, kvh, k0:k0 + 128], qT[:, h, qc0:qc0 + nq])
                    S.act(Pr[i % 4][:, 0:nq], ps[b][:, 0:nq], AF.Exp, scale=ATT_SCALE)
                j = i - LA
                if j >= 0:
                    k0, vc = chunks[j]
                    S.mm(ps[ob][:, 0:nq], Vst[:, vc, kvh * 128:(kvh + 1) * 128], Pr[j % 4][:, 0:nq], start=(j == 0), stop=(j == n - 1))
                    S.mm(ps[db][:, 0:nq], ones_b, Pr[j % 4][:, 0:nq], start=(j == 0), stop=(j == n - 1))
            def tail():
                S.op("dve", lambda hh: hh.reciprocal(rden[:, 0:nq], ps[db][:, 0:nq]), [ps[db][:, 0:nq]], [rden[:, 0:nq]])
                S.tt("dve", attnT[:, h, qc0:qc0 + nq], ps[ob][:, 0:nq], rden[:, 0:nq], ALU.mult)
            return tail

        def pool_segment(o, L, left_edge, right_edge, oc):
            W = L + 16
            for g in range(4):
                x = xpb[:, g, o:o + W]
                S.tt("pool", sA[:, 1:W], x[:, 0:W - 1], x[:, 1:W], ALU.add)
                if g == 0:
                    win = sA[:, 8:8 + L]
                elif g == 1:
                    S.tt("pool", sB[:, 8:8 + L], sA[:, 7:7 + L], sA[:, 9:9 + L], ALU.add)
                    win = sB[:, 8:8 + L]
                else:
                    S.tt("pool", sB[:, 3:W], sA[:, 1:W - 2], sA[:, 3:W], ALU.add)
                    if g == 2:
                        S.tt("pool", sC[:, 8:8 + L], sB[:, 7:7 + L], sB[:, 11:11 + L], ALU.add)
                        win = sC[:, 8:8 + L]
                    else:
                        S.tt("pool", sC[:, 7:W], sB[:, 3:W - 4], sB[:, 7:W], ALU.add)
                        S.tt("pool", sA[:, 8:8 + L], sC[:, 7:7 + L], sC[:, 15:15 + L], ALU.add)
                        win = sA[:, 8:8 + L]
                w = POOLW[g]
                S.stt("dve", pooledT[:, g, oc:oc + L], win, 1.0 / w, x[:, 8:8 + L], ALU.mult, ALU.subtract)
                hw = w // 2
                if left_edge:
                    for t in range(hw):
                        cnt = t + hw
                        S.stt("dve", pooledT[:, g, oc + t:oc + t + 1], win[:, t:t + 1], 1.0 / cnt, x[:, 8 + t:9 + t], ALU.mult, ALU.subtract)
                if right_edge:
                    for t in range(L - hw + 1, L):
                        cnt = L - t + hw
                        S.stt("dve", pooledT[:, g, oc + t:oc + t + 1], win[:, t:t + 1], 1.0 / cnt, x[:, 8 + t:9 + t], ALU.mult, ALU.subtract)

        def make_tile(l, g, src_d, row0, dst_d, st0, att_plan, xp_loads, pool_segs, is_out, nxt=None):
            sfx = "#L%dT%d" % (l, row0 // 512 if g == 0 else 8)

            def xres_load():
                load_x(src_d, row0)

            def prefetch_next():
                if nxt is not None:
                    load_x(nxt[0], nxt[1], dst=xnext)
            bctx = {"get": S.bank, "put": S.free}

            def xp_load():
                for dst, src in xp_loads:
                    S.dma(dst, src, queue="pool")
            wq = [None, None]

            def q_block(half, s, tset=0):
                S.tag = "q" + sfx
                if wq[half] is None:
                    wq[half] = wload(l, U_Q + half, 4096)[:, 0:4096].rearrange("p (a b) -> p a b", b=512)
                b = bctx["get"]()
                for kc in range(8):
                    S.mm(ps[b][:], hT[:, kc, s * 128:(s + 1) * 128], wq[half][:, kc, :], start=(kc == 0), stop=(kc == 7))
                rms_rope(ps[b][:], 4, qkg[:, 0, :], (st0 + s) if g == 0 else None, qr[:, s, half * 512:(half + 1) * 512], tset=tset)
                bctx["put"](b)

            def q_transposes(h0, h1):
                S.tag = "qT" + sfx
                for h in range(h0, h1):
                    b = bctx["get"]()
                    for s in range(4):
                        S.tr(psb[b][:, s * 128:(s + 1) * 128], qr[:, s, h * 128:(h + 1) * 128], ident_b)
                    S.copy("dve", qT[:, h, :], psb[b][:, 0:512])
                    bctx["put"](b)

            def sgu_proj(early=False):
                S.tag = "sgu" + sfx
                wxu = wload(l, U_XU, 4096, ahead=1 if early else 2)[:, 0:4096].rearrange("p (a b) -> p a b", b=512)
                for gg in range(4):
                    b = bctx["get"]()
                    for kc in range(8):
                        S.mm(ps[b][:], wxu[:, kc, gg * 128:(gg + 1) * 128], hT[:, kc, :], start=(kc == 0), stop=(kc == 7))
                    S.act(xuT[:, gg, :], ps[b][:], AF.Gelu)
                    bctx["put"](b)
                wxv = wload(l, U_XV, 4096, ahead=0 if early else 2)[:, 0:4096].rearrange("p (a b) -> p a b", b=512)
                for s in range(4):
                    b = bctx["get"]()
                    for kc in range(8):
                        S.mm(ps[b][:], hT[:, kc, s * 128:(s + 1) * 128], wxv[:, kc, :], start=(kc == 0), stop=(kc == 7))
                    S.act(xvt4[s], ps[b][:], AF.Gelu)
                    bctx["put"](b)

            def sgu_mix():
                S.tag = "sgu" + sfx
                for s in range(4):
                    xvt = xvt4[s]
                    S.op("dve", lambda hh, xvt=xvt: hh.bn_stats(bst[:, 0:6], xvt), [xvt], [bst[:, 0:6]])
                    S.op("dve", lambda hh: hh.bn_aggr(mv, bst[:, 0:6]), [bst[:, 0:6]], [mv])
                    S.act(rstd, mv[:, 1:2], AF.Ln, bias=eps_t[:, 0:1], scale=1.0)
                    S.act(rstd, rstd, AF.Exp, scale=-0.5)
                    S.ts("dve", vr[:, s, :], xvt, mv[:, 0:1], rstd, ALU.subtract, ALU.mult)
                for gg in range(4):
                    b = bctx["get"]()
                    for s in range(4):
                        S.mm(ps[b][:, s * 128:(s + 1) * 128], vr[:, s, gg * 128:(gg + 1) * 128], wsTb[:, gg, :])
                    S.tt("dve", stmp.rearrange("p (a b) -> p a b", b=128), ps[b][:].rearrange("p (a b) -> p a b", b=128),
                         bcast(bsgu[:, gg, :], pre=(4,)), ALU.add)
                    bctx["put"](b)
                    S.tt("dve", sguT[:, gg, :], stmp, xuT[:, gg, :], ALU.mult)

            def pool_stage():
                S.tag = "pool" + sfx
                for (o, L, le, re, oc) in pool_segs:
                    pool_segment(o, L, le, re, oc)
                for gg in range(4):
                    b = bctx["get"]()
                    S.mm(ps[b][:], wpgb[:, gg, :], pooledT[:, gg, :])
                    S.act(poolT[:, gg, :], ps[b][:], AF.Identity, bias=zero_t[:, 0:1], scale=pscT[:, l, gg:gg + 1])
                    bctx["put"](b)

            def _body(next_front, pending):
                pend = list(pending) if pending else [lambda: None, lambda: None]
                if g != 0:
                    pend[0](); pend[1]()
                    S.tag = "attn" + sfx
                    for (h, kvh, qc0, nq, chunks) in att_plan:
                        attention(h, kvh, qc0, nq, chunks)
                    xres_load()
                    sgu_mix(); xp_load(); pool_stage()
                    prefetch_next()
                else:
                    bs = [S.bank() for _ in range(8)]
                    misc = bs[7]
                    bctx["get"] = lambda: misc
                    bctx["put"] = lambda b: None
                    work = {0: [lambda: q_block(1, 1), pend[0]], 1: [lambda: q_block(1, 2), pend[1], xres_load], 2: [lambda: q_block(1, 3), wadvance],
                            3: [lambda: q_transposes(4, 8)], 4: [sgu_mix, xp_load], 5: [pool_stage, prefetch_next]}
                    for i, (h, kvh, qc0, nq, chunks) in enumerate(att_plan):
                        S.tag = "attn" + sfx
                        tail = attention2(h, kvh, qc0, nq, chunks, bs[2 * (i % 2)], bs[2 * (i % 2) + 1], bs[4:7])
                        for w in work.get(i, []):
                            w()
                        S.tag = "attn" + sfx
                        tail()
                    for b in bs:
                        S.free(b)
                    bctx["get"] = S.bank
                    bctx["put"] = S.free
                S.tag = "merge" + sfx
                for j in range(8):
                    wm = wload(l, U_MRG + j, 5120)[:, 0:5120].rearrange("p (a b) -> p a b", b=128)
                    gb = [S.bank() for _ in range(3)]
                    for i3 in range(3):
                        for kc in range(8):
                            S.mm(ps[gb[i3]][:], wm[:, i3 * 8 + kc, :], hT[:, kc, :], start=(kc == 0), stop=(kc == 7))
                        S.act(sig[i3], ps[gb[i3]][:], AF.Sigmoid)
                        S.free(gb[i3])
                    ob3 = [S.bank() for _ in range(3)]
                    for kc in range(8):
                        S.mm(ps[ob3[0]][:], wm[:, 24 + kc, :], attnT[:, kc, :], start=(kc == 0), stop=(kc == 7))
                    for kc in range(4):
                        S.mm(ps[ob3[1]][:], wm[:, 32 + kc, :], poolT[:, kc, :], start=(kc == 0), stop=(kc == 3))
                    for kc in range(4):
                        S.mm(ps[ob3[2]][:], wm[:, 36 + kc, :], sguT[:, kc, :], start=(kc == 0), stop=(kc == 3))
                    S.tt("dve", mtm, sig[0], ps[ob3[0]][:], ALU.mult)
                    S.tt("dve", mtt, sig[1], ps[ob3[1]][:], ALU.mult)
                    S.tt("pool", mtm, mtm, mtt, ALU.add)
                    S.tt("dve", mtt, sig[2], ps[ob3[2]][:], ALU.mult)
                    S.tt("dve", mergedT[:, j, :], mtm, mtt, ALU.add)
                    for b in ob3:
                        S.free(b)
                S.tag = "wout" + sfx
                S.dma(bcr[0], lnbc_d[l, 0], queue="pool"); S.dma(bcr[1], lnbc_d[l, 1], queue="pool")
                wo2 = [wload(l, U_WOUT + half, 4096, ahead=2 - half)[:, 0:4096].rearrange("p (a b) -> p a b", b=512) for half in range(2)]
                wt2 = [wtmp, sig[1]]
                for s in range(4):
                    S.tag = "wout" + sfx
                    for half in range(2):
                        b = S.bank()
                        for kc in range(8):
                            S.mm(ps[b][:], mergedT[:, kc, s * 128:(s + 1) * 128], wo2[half][:, kc, :], start=(kc == 0), stop=(kc == 7))
                        S.tt("dve", wt2[half], ps[b][:], gbc[g][0][:, half * 512:(half + 1) * 512], ALU.mult)
                        S.free(b)
                        xs_ = xt[:, s, half * 512:(half + 1) * 512]
                        S.stt("dve", xs_, xs_, ALPHA, wt2[half], ALU.mult, ALU.add)
                    layer_norm_sub(s)
                S.tag = "hT2" + sfx
                make_hT(g, 2, 3)
                for s in range(4):
                    S.tt("pool", xt[:, s, :], xt[:, s, :], bcr[0], ALU.mult)
                    S.tt("pool", xt[:, s, :], xt[:, s, :], bcr[1], ALU.add)
                S.tag = "ffnin" + sfx
                for m in range(11):
                    wf = wload(l, U_FIN + m, 4096)[:, 0:4096].rearrange("p (a b) -> p a b", b=512)
                    for jj in range(2):
                        ba = S.bank(); bb = S.bank()
                        for kc in range(8):
                            S.mm(ps[ba][:], wf[:, kc, jj * 128:(jj + 1) * 128], hT[:, kc, :], start=(kc == 0), stop=(kc == 7))
                        for kc in range(8):
                            S.mm(ps[bb][:], wf[:, kc, 256 + jj * 128:256 + (jj + 1) * 128], hT[:, kc, :], start=(kc == 0), stop=(kc == 7))
                        S.act(ftmp, ps[ba][:], AF.Silu)
                        S.free(ba)
                        S.tt("dve", uT[:, 2 * m + jj, :], ftmp, ps[bb][:], ALU.mult)
                        S.free(bb)
                S.tag = "ffnout" + sfx
                S.dma(bcr[0], lnbc_d[l, 2], queue="pool"); S.dma(bcr[1], lnbc_d[l, 3], queue="pool")
                acc = [[S.bank() for _ in range(2)] for _ in range(4)]
                for r in range(6):
                    nkk = 4 if r < 5 else 2
                    wo = wload(l, U_FOUT + r, nkk * 1024)[:, 0:nkk * 1024].rearrange("p (a b) -> p a b", b=1024)
                    for kk in range(nkk):
                        kc = 4 * r + kk
                        for s in range(4):
                            for half in range(2):
                                S.mm(ps[acc[s][half]][:], uT[:, kc, s * 128:(s + 1) * 128], wo[:, kk, half * 512:(half + 1) * 512],
                                     start=(kc == 0), stop=(kc == 21))
                for s in range(4):
                    for half in range(2):
                        S.tag = "ffnout" + sfx
                        S.tt("dve", ftmp, ps[acc[s][half]][:], gbc[g][1][:, half * 512:(half + 1) * 512], ALU.mult)
                        S.free(acc[s][half])
                        xs_ = xt[:, s, half * 512:(half + 1) * 512]
                        S.stt("dve", xs_, xs_, ALPHA, ftmp, ALU.mult, ALU.add)
                    if s == 0 and next_front is not None:
                        next_front[1]()
                def ln2_stats():
                    S.tag = "ln2" + sfx
                    for s in range(4):
                        S.op("dve", lambda hh, s=s: hh.bn_stats(bst4[:, s, 0:6], xt[:, s, 0:512]), [xt[:, s, 0:512]], [bst4[:, s, 0:6]])
                        S.op("dve", lambda hh, s=s: hh.bn_stats(bst4[:, s, 6:12], xt[:, s, 512:1024]), [xt[:, s, 512:1024]], [bst4[:, s, 6:12]])
                        S.op("dve", lambda hh, s=s: hh.bn_aggr(mv4[:, s, :], bst4[:, s, :]), [bst4[:, s, :]], [mv4[:, s, :]])

                def ln2_finish():
                    S.tag = "ln2" + sfx
                    S.act(rstd4, mv4[:, :, 1], AF.Ln, bias=eps_t[:, 0:1], scale=1.0)
                    S.act(rstd4, rstd4, AF.Exp, scale=-0.5)
                    S.stt("dve", nmr4, mv4[:, :, 0], -1.0, rstd4, ALU.mult, ALU.mult)
                    for s in range(4):
                        S.ts("dve", xt[:, s, :], xt[:, s, :], rstd4[:, s:s + 1], nmr4[:, s:s + 1], ALU.mult, ALU.add)
                        S.tt("pool", xt[:, s, :], xt[:, s, :], bcr[0], ALU.mult)
                        S.tt("pool", xt[:, s, :], xt[:, s, :], bcr[1], ALU.add)
                        S.dma(dst_d[row0 + s * 128: row0 + (s + 1) * 128, :], xt[:, s, :], queue="pool", is_output=is_out, track_out=not is_out)

                if next_front is not None:
                    next_front[2]()
                    return (ln2_stats, ln2_finish)
                ln2_stats(); ln2_finish()
                return None


            def front_a():
                S.tag = "hT1" + sfx
                make_hT(g, 0, 1, xsrc=xnext)

            def front_b():
                if g != 0:
                    for half in range(2):
                        for s in range(4):
                            q_block(half, s, tset=s % 2)
                    sgu_proj()
                    q_transposes(0, 8)
                else:
                    for s in range(4):
                        q_block(0, s, tset=s % 2)
                    q_block(1, 0, tset=0)
                    sgu_proj(early=True)
                    q_transposes(0, 4)

            def front():
                front_a(); front_b()

            def body(next_front=None, pending=None):
                return _body(next_front, pending)
            return (front, front_a, front_b), body

        def layer_norm_tile():
            for s in range(4):
                layer_norm_sub(s)

        def layer_norm_sub(s):
            if True:
                S.op("dve", lambda hh, s=s: hh.bn_stats(bst[:, 0:6], xt[:, s, 0:512]), [xt[:, s, 0:512]], [bst[:, 0:6]])
                S.op("dve", lambda hh, s=s: hh.bn_stats(bst[:, 6:12], xt[:, s, 512:1024]), [xt[:, s, 512:1024]], [bst[:, 6:12]])
                S.op("dve", lambda hh: hh.bn_aggr(mv, bst), [bst], [mv])
                S.act(rstd, mv[:, 1:2], AF.Ln, bias=eps_t[:, 0:1], scale=1.0)
                S.act(rstd, rstd, AF.Exp, scale=-0.5)
                S.stt("dve", nmr, mv[:, 0:1], -1.0, rstd, ALU.mult, ALU.mult)
                S.ts("dve", xt[:, s, :], xt[:, s, :], rstd, nmr, ALU.mult, ALU.add)

        if "prep" in phases:
            prep_all([0, 1])
        S.barrier()
        for l in range(2):
            if ("l%d" % l) not in phases:
                continue
            if (dbg or {}).get('setup', True):
                layer_setup(l, (dbg or {}).get('parts', 'ABCD'))
            src_s, dst_s = (xs_d, x1s_d) if l == 0 else (x1s_d, ys_d)
            src_p, dst_p = (xp_d, x1p_d) if l == 0 else (x1p_d, yp_d)
            if dbg_l0_out and l == 0:
                dst_s, dst_p = ys_d, yp_d
            last = (l == 1) or dbg_l0_out
            wkv = wring[0][:, 0:4096].rearrange("p (a b) -> p a b", b=512)
            wpin = wring[1][:, 0:4096].rearrange("p (a b) -> p a b", b=512)
            S.dma(wring[0][:, 0:4096], wsc_d[l][U_KV, :, 0:4096])
            S.dma(wring[1][:, 0:4096], wsc_d[l][U_PIN, :, 0:4096])
            ntl = (dbg or {}).get("n_main_s", NSAMP_T) + (1 if (dbg or {}).get("main_p", True) else 0)
            wplan_start(l, ntl, 2)
            dbg_ = dbg or {}
            if dbg_.get("cache", True):
                load_cache(l)
            nps = dbg_.get("n_pre_s", NSAMP_T)
            do_pp = dbg_.get("pre_p", True)
            xb = [xt, xnext]
            pre = [(src_s, T * 512) for T in range(nps)] + ([(src_p, 0)] if do_pp else [])
            if pre:
                load_x(pre[0][0], pre[0][1], dst=xb[0])
            for k, (sd, r0) in enumerate(pre):
                nx = (pre[k + 1][0], pre[k + 1][1], xb[(k + 1) % 2]) if k + 1 < len(pre) else None
                if sd is src_s:
                    T = k
                    prepass_tile(l, 0, src_s, T * 512, 512 + T * 512, 4 + T * 4, T * 4, wkv, wpin,
                                 lambda gg, T=T: [(xps_d[gg, :, 8 + T * 512: 8 + T * 512 + 512], 0, 512)], xbuf=xb[k % 2], nxt=nx)
                else:
                    prepass_tile(l, 1, src_p, 0, KOFF_P, 36, None, wkv, wpin,
                                 lambda gg: [(xpp_d[gg, :, 0, 8:264], 0, 256), (xpp_d[gg, :, 1, 8:264], 256, 512)],
                                 nk_out=lambda s, l=l: nk_d[s // 2, l, (s % 2) * 128:(s % 2) * 128 + 128, :],
                                 nv_out=lambda s, l=l: nv_d[s // 2, l, (s % 2) * 128:(s % 2) * 128 + 128, :], xbuf=xb[k % 2], nxt=nx)
            nms = dbg_.get("n_main_s", NSAMP_T)
            do_p = dbg_.get("main_p", True)
            if nms > 0:
                load_x(src_s, 0, dst=xnext)
            elif do_p:
                load_x(src_p, 0, dst=xnext)
            tiles = []
            for T in range(nms):
                chunks = [(kc * 128, kc) for kc in range(36)]
                plan = [(h, h // 4, 0, 512, chunks) for h in range(8)]
                xp_loads = [(xpb[:, gg, 0:528], xps_d[gg, :, T * 512: T * 512 + 528]) for gg in range(4)]
                segs = [(0, 512, T == 0, T == NSAMP_T - 1, 0)]
                nxt = (src_s, (T + 1) * 512) if T + 1 < nms else ((src_p, 0) if do_p else None)
                tiles.append(make_tile(l, 0, src_s, T * 512, dst_s, T * 4, plan, xp_loads, segs, last, nxt=nxt))
            if do_p:
                plan = []
                for sq_ in range(2):
                    chunks = [(KOFF_P + sq_ * 256 + i * 128, 36 + sq_ * 2 + i) for i in range(2)]
                    for h in range(8):
                        plan.append((h, h // 4, sq_ * 256, 256, chunks))
                xp_loads = [(xpb[:, gg, :].rearrange("p (a b) -> p a b", b=272), xpp_d[gg]) for gg in range(4)]
                segs = [(0, 256, True, True, 0), (272, 256, True, True, 256)]
                tiles.append(make_tile(l, 1, src_p, 0, dst_p, None, plan, xp_loads, segs, last))
            if tiles:
                tiles[0][0][0]()
            pending = None
            for ti, (fr, bd) in enumerate(tiles):
                pending = bd(tiles[ti + 1][0] if ti + 1 < len(tiles) else None, pending)
        S.emit(es)
    return nc, S


def _rope_tables():
    inv = (np.float32(10000.0) ** (-np.arange(32, dtype=np.float32) / np.float32(32))).astype(np.float32)
    p = np.arange(128)
    ropeR = np.zeros((128, 2, 32, 32), np.float32)
    for st in range(32):
        row = (2 * st + p // 64).astype(np.float32)
        ang = (row[:, None] * inv[None, :]).astype(np.float32)
        ropeR[:, 0, st, :] = np.cos(ang)
        ropeR[:, 1, st, :] = np.sin(ang)
    col = (p % 64).astype(np.float32)
    ang = (col[:, None] * inv[None, :]).astype(np.float32)
    ropeC = np.stack([np.cos(ang), np.sin(ang)], axis=1).astype(np.float32)
    return ropeR, ropeC


_CACHE = {}


def make_in_maps(x_prompt, x_sample, cache_k, cache_v, c, c_ctx, w_ada, b_ada, w_in, q_norm_g, k_norm_g,
                 w_pool_g, pool_scale, w_sgu, b_sgu, w_attn_o, w_pool_o, w_sgu_o, w_out, ln1_g, ln1_b,
                 w_ffn_in, w_ffn_out, ln2_g, ln2_b):
    f = lambda a: np.ascontiguousarray(np.asarray(a, dtype=np.float32))
    x_prompt, x_sample, cache_k, cache_v, c, c_ctx = map(f, (x_prompt, x_sample, cache_k, cache_v, c, c_ctx))
    ropeR, ropeC = _rope_tables()
    bc = lambda v, n=128: np.ascontiguousarray(np.broadcast_to(v, (n,) + v.shape))
    qkg = np.stack([f(q_norm_g), f(k_norm_g)], axis=1)
    qkg = np.ascontiguousarray(np.broadcast_to(qkg[:, None], (2, 128, 2, 128)))
    ln = np.stack([f(ln1_g), f(ln1_b), f(ln2_g), f(ln2_b)], axis=1)
    lnbc = np.ascontiguousarray(np.broadcast_to(ln[:, :, None, :], (2, 4, 128, 1024)))
    lnT = np.ascontiguousarray(ln[:, 0:2].reshape(2, 2, 8, 128).transpose(3, 0, 1, 2))
    shared = dict(
        w_ada=f(w_ada), b_adaT=np.ascontiguousarray(f(b_ada).reshape(2, 48, 128).transpose(2, 0, 1)),
        w_in=f(w_in), w_attn_o=f(w_attn_o), w_pool_o=f(w_pool_o), w_sgu_o=f(w_sgu_o), w_out=f(w_out),
        w_ffn_in=f(w_ffn_in), w_ffn_out=f(w_ffn_out),
        w_pg=np.ascontiguousarray(f(w_pool_g).transpose(0, 2, 1, 3)),
        w_sT=np.ascontiguousarray(f(w_sgu).transpose(0, 3, 1, 2)),
        bsgu=np.ascontiguousarray(np.broadcast_to(f(b_sgu)[:, None], (2, 128, 4, 128))),
        pscT=np.ascontiguousarray(f(pool_scale).reshape(2, 4, 128).transpose(2, 0, 1)),
        qkg=qkg, lnbc=lnbc, lnT=lnT, ropeR=ropeR, ropeC=ropeC, idf=np.eye(128, dtype=np.float32),
    )
    in_maps = []
    for i in range(8):
        m = dict(shared)
        m["xs"] = x_sample[i]
        m["xp"] = np.ascontiguousarray(x_prompt[2 * i:2 * i + 2].reshape(512, 1024))
        m["ck"] = np.ascontiguousarray(cache_k[i].reshape(2, 512, 256))
        m["cv"] = np.ascontiguousarray(cache_v[i].reshape(2, 512, 256))
        cT = np.stack([c[i].reshape(8, 128).T, c_ctx.reshape(8, 128).T], axis=2)
        m["cT"] = np.ascontiguousarray(cT.astype(np.float32))
        in_maps.append(m)
    return in_maps


def kernel(**inputs):
    if "nc" not in _CACHE:
        _CACHE["nc"] = build_program()[0]
    nc = _CACHE["nc"]
    in_maps = make_in_maps(**inputs)
    res = run_bass_kernel_spmd(nc, in_maps, core_ids=list(range(8)))
    y_p = np.zeros((16, 256, 1024), np.float32)
    y_s = np.zeros((8, 4096, 1024), np.float32)
    nk = np.zeros((16, 2, 256, 2, 128), np.float32)
    nv = np.zeros((16, 2, 256, 2, 128), np.float32)
    for i in range(8):
        r = res.results[i]
        y_s[i] = r["ys"]
        y_p[2 * i:2 * i + 2] = r["yp"].reshape(2, 256, 1024)
        nk[2 * i:2 * i + 2] = r["nk"].reshape(2, 2, 256, 2, 128)
        nv[2 * i:2 * i + 2] = r["nv"].reshape(2, 2, 256, 2, 128)
    return (y_p, y_s, nk, nv)
```

```python
import numpy as np
from concourse.bass_utils import run_bass_kernel_spmd
from contextlib import ExitStack
import concourse.bass as bass
import concourse.mybir as mybir

F32 = mybir.dt.float32
BF16 = mybir.dt.bfloat16
ALU = mybir.AluOpType
AF = mybir.ActivationFunctionType
AX = mybir.AxisListType
_ESZ = {F32: 4, BF16: 2}
ENGS = ("pe", "act", "dve", "pool", "sp")
RING = 16
EPOCH = 1024


def _esz(dt):
    if dt in _ESZ:
        return _ESZ[dt]
    return mybir.dt.size(dt) if hasattr(mybir.dt, "size") else 4


class Sched:
    def __init__(self, nc):
        self.nc = nc
        self.q = {e: [] for e in ENGS}
        self.acc = {}
        self.dma_cnt = {e: 0 for e in ENGS}
        self.dmas = []
        self.out_dmas = []
        self.banks_free = list(range(8))
        self._slot_last = {}
        self.cells = {}
        self._csz = {}

    def box(self, ap):
        t = ap.tensor
        es = _esz(ap.dtype)
        dims = ap.ap
        sp = str(ap.space)
        if "DRAM" in sp.upper() or "HBM" in sp.upper():
            ext = sum((c - 1) * abs(s) for s, c in dims) + 1
            return ("DR:" + t.name, 0, 1, ap.offset * es, (ap.offset + ext) * es)
        if "PSUM" in sp.upper():
            return (t.name, 0, 128, 0, 2048)
        shp = list(t.shape)
        psz = 1
        for v in shp[1:]:
            psz *= v
        psz_b = psz * _esz(t.dtype)
        off_b = ap.offset * es
        p0 = off_b // psz_b
        f0 = off_b % psz_b
        npart = dims[0][1]
        ext = sum((c - 1) * abs(s) for s, c in dims[1:]) + 1
        return (t.name, p0, p0 + npart, f0, f0 + ext * es)

    @staticmethod
    def _ov(a, b):
        return a[1] < b[2] and b[1] < a[2] and a[3] < b[4] and b[3] < a[4]

    @staticmethod
    def _contains(a, b):
        return a[1] <= b[1] and a[2] >= b[2] and a[3] <= b[3] and a[4] >= b[4]

    def _access(self, ap, is_write, ref, deps, eng):
        bx = self.box(ap)
        name = bx[0]
        ent = self.acc.setdefault(name, {})
        cells = self.cells.setdefault(name, {})
        csz = 1024 if (bx[2] - bx[1] > 1 or bx[4] < (1 << 20)) and not name.startswith("DR:") else (1 << 18)
        if name not in self._csz:
            self._csz[name] = csz
        csz = self._csz[name]
        crange = range(bx[3] // csz, (bx[4] - 1) // csz + 1)
        cand = set()
        for c in crange:
            s = cells.get(c)
            if s:
                cand |= s
        for ob in cand:
            if not self._ov(bx, ob):
                continue
            rec = ent[ob]
            if rec[0] is not None:
                deps.append((rec[0], "waw" if is_write else "raw"))
            if is_write:
                for r in rec[1].values():
                    deps.append((r, "war"))
        if is_write:
            for ob in [ob for ob in cand if self._contains(bx, ob)]:
                del ent[ob]
                for c in range(ob[3] // csz, (ob[4] - 1) // csz + 1):
                    cells[c].discard(ob)
            ent[bx] = [ref, {}]
            for c in crange:
                cells.setdefault(c, set()).add(bx)
        else:
            rec = ent.get(bx)
            if rec is None:
                rec = ent[bx] = [None, {}]
                for c in crange:
                    cells.setdefault(c, set()).add(bx)
            rec[1][eng if ref[0] == "e" else ref] = ref

    def op(self, eng, fn, reads=(), writes=(), track=True):
        idx = len(self.q[eng])
        ref = ("e", eng, idx)
        deps = []
        for ap in reads:
            if ap is not None and not isinstance(ap, (int, float)):
                self._access(ap, False, ref, deps, eng)
        for ap in writes:
            if ap is not None:
                self._access(ap, True, ref, deps, eng)
        self.q[eng].append(dict(fn=fn, deps=deps, inc=False, kind="op", tag=getattr(self, "tag", "?")))
        return ref

    def dma(self, out, in_, queue="sp", track_in=True, track_out=True, is_output=False, **kw):
        n = self.dma_cnt[queue]
        self.dma_cnt[queue] = n + 1
        slot = n % RING
        cum = 16 * (n // RING + 1)
        did = len(self.dmas)
        self.dmas.append((queue, slot, cum))
        ref = ("d", did)
        deps = []
        if n >= RING:
            prev = self._slot_last[(queue, slot)]
            deps.append((("d", prev), "ring"))
        self._slot_last[(queue, slot)] = did
        if track_in:
            self._access(in_, False, ref, deps, queue)
        if track_out:
            self._access(out, True, ref, deps, queue)
        if is_output:
            self.out_dmas.append(did)
        self.q[queue].append(dict(fn=None, deps=deps, inc=False, kind="dma", out=out, in_=in_, did=did, kw=kw))
        return ref

    def barrier(self):
        last = {e: len(self.q[e]) - 1 for e in ENGS}
        ndma = len(self.dmas)
        for e in ENGS:
            deps = []
            for e2 in ENGS:
                if e2 != e and last[e2] >= 0:
                    k = last[e2]
                    while k >= 0 and self.q[e2][k]["kind"] != "op":
                        k -= 1
                    if k >= 0:
                        deps.append((("e", e2, k), "raw"))
            seenq = {}
            for d in range(ndma - 1, -1, -1):
                qn = self.dmas[d][0]
                if seenq.get(qn, 0) < RING:
                    seenq[qn] = seenq.get(qn, 0) + 1
                    deps.append((("d", d), "raw"))
            self.q[e].append(dict(fn=None, deps=deps, inc=False, kind="nop"))

    def bank(self):
        return self.banks_free.pop(0)

    def free(self, b):
        self.banks_free.append(b)

    def emit(self, es):
        nc = self.nc
        for e in ENGS:
            for op in self.q[e]:
                keep = []
                best = {}
                for ref, kind in op["deps"]:
                    if ref[0] == "e":
                        if ref[1] == e and (e == "pe" or kind != "raw"):
                            continue
                        if ref[2] > best.get(ref[1], -1):
                            best[ref[1]] = ref[2]
                    else:
                        keep.append(ref)
                for e2, k in best.items():
                    self.q[e2][k]["inc"] = True
                    keep.append(("e", e2, k))
                op["deps"] = keep
        cnt = {}
        for e in ENGS:
            c = 0
            for i, op in enumerate(self.q[e]):
                if op["inc"]:
                    c += 1
                cnt[(e, i)] = c
        tot = {e: (cnt[(e, len(self.q[e]) - 1)] if self.q[e] else 0) for e in ENGS}
        sem_ep = {e: [es.enter_context(nc.semaphore("s_%s_%d" % (e, k))) for k in range(max(1, (tot[e] + EPOCH - 1) // EPOCH))] for e in ENGS}
        dq = [e for e in ENGS if self.dma_cnt[e] > 0]
        sem_d = {(e, s): es.enter_context(nc.semaphore("d_%s_%d" % (e, s))) for e in dq for s in range(min(RING, self.dma_cnt[e]))}
        handles = {"pe": nc.tensor, "act": nc.scalar, "dve": nc.vector, "pool": nc.gpsimd, "sp": nc.sync}
        stats = {e: [0, 0, 0] for e in ENGS}

        def emit_eng(e, h):
            seen_e = {}
            seen_d = {}
            for i, op in enumerate(self.q[e]):
                need_e = {}
                need_d = {}
                for ref in op["deps"]:
                    if ref[0] == "e":
                        v = cnt[(ref[1], ref[2])]
                        if v > seen_e.get(ref[1], 0) and v > need_e.get(ref[1], 0):
                            need_e[ref[1]] = v
                    else:
                        qn, slot, cum = self.dmas[ref[1]]
                        if cum > seen_d.get((qn, slot), 0) and cum > need_d.get((qn, slot), 0):
                            need_d[(qn, slot)] = cum
                waits = [(sem_ep[k][(v - 1) // EPOCH], (v - 1) % EPOCH + 1) for k, v in need_e.items()]
                waits += [(sem_d[k], v) for k, v in need_d.items()]
                for k, v in need_e.items():
                    seen_e[k] = v
                for k, v in need_d.items():
                    seen_d[k] = v
                emb = None
                if op["kind"] == "op" and waits:
                    emb = waits.pop()
                for sm, v in waits:
                    h.wait_ge(sm, v)
                    stats[e][1] += 1
                if op["kind"] == "op":
                    ins = op["fn"](h)
                    if emb is not None:
                        ins._wait_ge(emb[0], emb[1])
                    stats[e][0] += 1
                    if op["inc"]:
                        ins.then_inc(sem_ep[e][(cnt[(e, i)] - 1) // EPOCH], 1)
                        stats[e][2] += 1
                elif op["kind"] == "dma":
                    qn, slot, cum = self.dmas[op["did"]]
                    h.dma_start(out=op["out"], in_=op["in_"], **op["kw"]).then_inc(sem_d[(qn, slot)], 16)
                    stats[e][0] += 1
            if e == "sp":
                fin = {}
                for d in self.out_dmas:
                    qn, slot, cum = self.dmas[d]
                    fin[(qn, slot)] = max(fin.get((qn, slot), 0), cum)
                for k, v in fin.items():
                    if v > seen_d.get(k, 0):
                        h.wait_ge(sem_d[k], v)

        with nc.Block() as block:
            @block.tensor
            def _(h):
                emit_eng("pe", h)

            @block.scalar
            def _(h):
                emit_eng("act", h)

            @block.vector
            def _(h):
                emit_eng("dve", h)

            @block.gpsimd
            def _(h):
                emit_eng("pool", h)

            @block.sync
            def _(h):
                emit_eng("sp", h)
        self.stats = stats

    def mm(self, out, lhsT, rhs, start=True, stop=True):
        return self.op("pe", lambda h: h.matmul(out, lhsT, rhs, start=start, stop=stop), [lhsT, rhs], [out])

    def tr(self, out, in_, ident):
        return self.op("pe", lambda h: h.transpose(out, in_, ident), [in_, ident], [out])

    def act(self, out, in_, func, bias=None, scale=None, accum_out=None, eng="act"):
        kw = {}
        if bias is not None:
            kw["bias"] = bias
        if scale is not None:
            kw["scale"] = scale
        if accum_out is not None:
            kw["accum_out"] = accum_out
        rd = [in_] + [a for a in (bias, scale) if a is not None and not isinstance(a, (int, float))]
        wr = [out] + ([accum_out] if accum_out is not None else [])
        return self.op("act", lambda h: h.activation(out, in_, func, **kw), rd, wr)

    def tt(self, eng, out, in0, in1, op):
        return self.op(eng, lambda h: h.tensor_tensor(out, in0, in1, op), [in0, in1], [out])

    def ts(self, eng, out, in0, s1, s2, op0, op1=None):
        rd = [in0] + [a for a in (s1, s2) if a is not None and not isinstance(a, (int, float))]
        if op1 is None:
            name = {ALU.add: "tensor_scalar_add", ALU.mult: "tensor_scalar_mul", ALU.subtract: "tensor_scalar_sub"}[op0]
            return self.op(eng, lambda h: getattr(h, name)(out, in0, s1), rd, [out])
        return self.op(eng, lambda h: h.tensor_scalar(out, in0, s1, s2, op0, op1), rd, [out])

    def stt(self, eng, out, in0, scalar, in1, op0, op1):
        rd = [in0, in1] + ([scalar] if not isinstance(scalar, (int, float)) else [])
        return self.op(eng, lambda h: h.scalar_tensor_tensor(out, in0, scalar, in1, op0, op1), rd, [out])

    def copy(self, eng, out, in_):
        if eng == "act":
            return self.op("act", lambda h: h.copy(out, in_), [in_], [out])
        return self.op(eng, lambda h: h.tensor_copy(out, in_), [in_], [out])

    def memset(self, eng, ap, val):
        return self.op(eng, lambda h: h.memset(ap, val), [], [ap])

D = 1024
DEPTH = 2
ALPHA = float((2 * DEPTH) ** 0.25)
EPS = 1e-6
NSAMP_T = 8
KOFF_P = 4608
ATT_SCALE = float(128 ** -0.5)
NUNITS = 33
UNIT = 5120
U_KV, U_PIN, U_Q, U_XU, U_XV, U_MRG, U_WOUT, U_FIN, U_FOUT = 0, 1, 2, 4, 5, 6, 14, 16, 27
POOLW = (2, 4, 8, 16)


def bcast(ap, pre=(), post=()):
    dims = [list(ap.ap[0])] + [[0, n] for n in pre] + [list(d) for d in ap.ap[1:]] + [[0, n] for n in post]
    return bass.AP(ap.tensor, ap.offset, dims)


def build_program(phases=("prep", "l0", "l1"), dbg_l0_out=False, dbg=None):
    nc = bass.Bass("TRN2", target_bir_lowering=False)

    def din(name, shape, dt=F32):
        return nc.dram_tensor(name, list(shape), dt, kind="ExternalInput").ap()

    def dout(name, shape):
        return nc.dram_tensor(name, list(shape), F32, kind="ExternalOutput").ap()

    def dscr(name, shape, dt=F32):
        return nc.dram_tensor(name, list(shape), dt, kind="Internal").ap()

    xs_d = din("xs", [4096, D]); xp_d = din("xp", [512, D])
    ck_d = din("ck", [2, 512, 256]); cv_d = din("cv", [2, 512, 256])
    cT_d = din("cT", [128, 8, 2])
    wada_d = din("w_ada", [2, D, 6 * D]); badaT_d = din("b_adaT", [128, 2, 48])
    win_d = din("w_in", [2, D, 6144]); wao_d = din("w_attn_o", [2, D, D])
    wpo_d = din("w_pool_o", [2, 512, D]); wso_d = din("w_sgu_o", [2, 512, D])
    wout_d = din("w_out", [2, D, D]); wfi_d = din("w_ffn_in", [2, D, 5632]); wfo_d = din("w_ffn_out", [2, 2816, D])
    wpg_d = din("w_pg", [2, 128, 4, 128]); wsT_d = din("w_sT", [2, 128, 4, 128])
    bsgu_d = din("bsgu", [2, 128, 4, 128]); pscT_d = din("pscT", [128, 2, 4])
    qkg_d = din("qkg", [2, 128, 2, 128]); lnbc_d = din("lnbc", [2, 4, 128, D]); lnT_d = din("lnT", [128, 2, 2, 8])
    ropeR_d = din("ropeR", [128, 2, 32, 32]); ropeC_d = din("ropeC", [128, 2, 32]); idf_d = din("idf", [128, 128])
    ys_d = dout("ys", [4096, D]); yp_d = dout("yp", [512, D])
    nk_d = dout("nk", [2, 2, 256, 256]); nv_d = dout("nv", [2, 2, 256, 256])
    x1s_d = dscr("x1s", [4096, D]); x1p_d = dscr("x1p", [512, D])
    xps_d = dscr("xps", [4, 128, 4112]); xpp_d = dscr("xpp", [4, 128, 2, 272])
    wsc_d = [dscr("wsc%d" % l, [NUNITS, 128, UNIT], BF16) for l in range(2)]

    S = Sched(nc)
    es = ExitStack()
    with es:
        AW = 52736
        arena = es.enter_context(nc.sbuf_tensor("arena", [128, AW], F32))
        ps = [es.enter_context(nc.psum_tensor("ps%d" % i, [128, 512], F32)) for i in range(8)]
        psb = [p[:].bitcast(BF16) for p in ps]

        def carve(off_b, shape, dt):
            n = 1
            for v in shape[1:]:
                n *= v
            nb = n * (4 if dt == F32 else 2)
            assert off_b % 4 == 0 and off_b + nb <= AW * 4, (off_b, nb)
            a = arena[:, off_b // 4:(off_b + nb + 3) // 4]
            if dt != F32:
                a = a.bitcast(dt)
            if len(shape) == 3:
                a = a.rearrange("p (a b) -> p a b", b=shape[2])
            elif len(shape) == 4:
                a = a.rearrange("p (a b c) -> p a b c", b=shape[2], c=shape[3])
            return a

        cur = [0]

        def palloc(shape, dt):
            n = 1
            for v in shape[1:]:
                n *= v
            nb = (n * (4 if dt == F32 else 2) + 3) // 4 * 4
            a = carve(cur[0], shape, dt)
            cur[0] += nb
            return a

        KT = palloc([128, 2, 5120], BF16)
        Vst = palloc([128, 40, 256], BF16)
        ident_f = palloc([128, 128], F32); ident_b = palloc([128, 128], BF16)
        ones_f = palloc([128, 128], F32); ones_b = palloc([128, 128], BF16)
        ropeR = palloc([128, 2, 32, 32], F32); ropeC = palloc([128, 2, 32], F32)
        qkg = palloc([128, 2, 128], F32); bsgu = palloc([128, 4, 128], F32)
        pscT = palloc([128, 2, 4], F32); lnT = palloc([128, 2, 2, 8], F32)
        badaT = palloc([128, 2, 48], F32); sil = palloc([128, 8, 2], F32)
        mod = palloc([128, 48, 2], F32); scal = palloc([128, 2, 4, 8], F32)
        wpgb = palloc([128, 4, 128], BF16); wsTb = palloc([128, 4, 128], BF16)
        small = palloc([128, 64], F32)
        small2 = palloc([128, 64], F32)
        zero_t = palloc([128, 16], F32)
        eps_t = palloc([128, 16], F32)
        gbc = [[palloc([128, D], F32) for _ in range(2)] for _ in range(2)]
        bcr = [palloc([128, D], F32) for _ in range(2)]
        span0 = cur[0]
        wring = [palloc([128, UNIT], BF16) for _ in range(3)]
        hT = palloc([128, 8, 512], BF16)
        xt = palloc([128, 4, D], F32)
        P0 = cur[0]
        PA = 38912
        assert P0 + PA <= AW * 4, (P0, PA, AW * 4)
        R = P0 + 16384
        attnT = carve(P0, [128, 8, 512], BF16)
        sguT = carve(P0 + 8192, [128, 4, 512], BF16)
        poolT = carve(P0 + 12288, [128, 4, 512], BF16)
        qr = carve(R, [128, 4, D], BF16); qT = carve(R + 8192, [128, 8, 512], BF16)
        nsq = carve(R + 16384, [128, 512], F32); nqn = carve(R + 18432, [128, 512], F32); nB = carve(R + 20480, [128, 512], F32)
        N0 = P0 + PA
        X = N0 + 6144
        assert X + 22528 <= AW * 4, (X, AW * 4)
        Pr = [carve(N0 + 1024 * i, [128, 512], BF16) for i in range(4)]
        rden = carve(N0 + 4096, [128, 512], F32)
        tsetB = [carve(N0 + 2048 * i, [128, 512], F32) for i in range(3)]
        xuT = carve(X, [128, 4, 512], F32); vr = carve(X + 8192, [128, 4, 512], BF16)
        xvt4 = [carve(X + 12288 + 2048 * i, [128, 512], F32) for i in range(4)]
        stmp = carve(X + 20480, [128, 512], F32)
        xnext = carve(X, [128, 4, D], F32)
        xpb = carve(X, [128, 4, 544], F32)
        sA = carve(X + 8704, [128, 544], F32); sB = carve(X + 8704 + 2176, [128, 544], F32); sC = carve(X + 8704 + 4352, [128, 544], F32)
        pooledT = carve(X + 15232, [128, 4, 512], BF16)
        sig = [carve(R + 2048 * i, [128, 512], F32) for i in range(3)]
        mtm = carve(R + 6144, [128, 512], F32); mtt = carve(R + 8192, [128, 512], F32)
        mergedT = carve(R + 10240, [128, 8, 512], BF16)
        wtmp = carve(R, [128, 512], F32)
        uT = carve(P0, [128, 22, 512], BF16)
        ftmp = carve(P0 + 22528, [128, 512], F32)
        kr = carve(R, [128, 4, 256], BF16); vtmp = carve(R + 2048, [128, 256], F32)
        xpstage = carve(R + 4096, [128, 4, 512], F32)
        cstage = carve(R, [128, 4, 256], F32); cstb = carve(R + 4096, [128, 4, 256], BF16)
        st32 = [carve(span0 + 20480 * i, [128, UNIT], F32) for i in range(3)]
        st16 = [carve(span0 + 61440 + 10240 * i, [128, UNIT], BF16) for i in range(3)]
        ss = small[:, 0:8]; bst = small[:, 8:20]; mv = small[:, 20:22]; rstd = small[:, 22:23]; nmr = small[:, 23:24]
        ss2 = [small[:, 0:8], small[:, 24:32]]
        bst4 = small2[:, 0:48].rearrange("p (a b) -> p a b", b=12); mv4 = small2[:, 48:56].rearrange("p (a b) -> p a b", b=2)
        rstd4 = small2[:, 56:60]; nmr4 = small2[:, 60:64]

        S.dma(ident_f, idf_d)
        S.dma(ropeR, ropeR_d); S.dma(ropeC, ropeC_d)
        S.dma(pscT, pscT_d); S.dma(lnT, lnT_d); S.dma(badaT, badaT_d)
        S.dma(sil, cT_d)
        S.copy("dve", ident_b, ident_f)
        S.memset("dve", ones_f, 1.0); S.memset("dve", ones_b, 1.0)
        S.memset("dve", zero_t, 0.0)
        S.memset("dve", eps_t, EPS)
        S.act(sil, sil, AF.Silu)
        for g in range(4):
            S.dma(xps_d[g, :, 0:8], zero_t[:, 0:8]); S.dma(xps_d[g, :, 4104:4112], zero_t[:, 0:8])
            for sq_ in range(2):
                S.dma(xpp_d[g, :, sq_, 0:8], zero_t[:, 0:8]); S.dma(xpp_d[g, :, sq_, 264:272], zero_t[:, 0:8])

        def wsrc(w, l, k0, k1, n0, n1):
            return w[l].rearrange("(c p) n -> p c n", p=128)[:, k0:k1, n0:n1]

        def unit_pieces(l, u):
            if u == U_KV:
                return [(wsrc(win_d, l, 0, 8, 1024, 1536), (0, 8, 0, 512))], 8, 512
            if u == U_PIN:
                return [(wsrc(win_d, l, 0, 8, 1536, 2048), (0, 8, 0, 512))], 8, 512
            if u in (U_Q, U_Q + 1):
                h = u - U_Q
                return [(wsrc(win_d, l, 0, 8, h * 512, h * 512 + 512), (0, 8, 0, 512))], 8, 512
            if u == U_XU:
                return [(wsrc(win_d, l, 0, 8, 2048, 2560), (0, 8, 0, 512))], 8, 512
            if u == U_XV:
                return [(wsrc(win_d, l, 0, 8, 2560, 3072), (0, 8, 0, 512))], 8, 512
            if U_MRG <= u < U_MRG + 8:
                j = u - U_MRG
                c0, c1 = j * 128, j * 128 + 128
                return [(wsrc(win_d, l, 0, 8, 3072 + c0, 3072 + c1), (0, 8, 0, 128)),
                        (wsrc(win_d, l, 0, 8, 4096 + c0, 4096 + c1), (8, 16, 0, 128)),
                        (wsrc(win_d, l, 0, 8, 5120 + c0, 5120 + c1), (16, 24, 0, 128)),
                        (wsrc(wao_d, l, 0, 8, c0, c1), (24, 32, 0, 128)),
                        (wsrc(wpo_d, l, 0, 4, c0, c1), (32, 36, 0, 128)),
                        (wsrc(wso_d, l, 0, 4, c0, c1), (36, 40, 0, 128))], 40, 128
            if u in (U_WOUT, U_WOUT + 1):
                h = u - U_WOUT
                return [(wsrc(wout_d, l, 0, 8, h * 512, h * 512 + 512), (0, 8, 0, 512))], 8, 512
            if U_FIN <= u < U_FIN + 11:
                m = u - U_FIN
                return [(wsrc(wfi_d, l, 0, 8, m * 256, m * 256 + 256), (0, 8, 0, 256)),
                        (wsrc(wfi_d, l, 0, 8, 2816 + m * 256, 2816 + m * 256 + 256), (0, 8, 256, 512))], 8, 512
            r = u - U_FOUT
            k0 = 4 * r
            k1 = min(22, k0 + 4)
            return [(wsrc(wfo_d, l, k0, k1, 0, 1024), (0, k1 - k0, 0, 1024))], k1 - k0, 1024

        def prep_all(layers):
            jobs = [(l, u) for l in layers for u in range(NUNITS)]

            def issue_in(k):
                l, u = jobs[k]
                pieces, nk, ncols = unit_pieces(l, u)
                v32 = st32[k % 3][:, 0:nk * ncols].rearrange("p (a b) -> p a b", b=ncols)
                for src, (k0, k1, c0, c1) in pieces:
                    S.dma(v32[:, k0:k1, c0:c1], src, track_in=False)

            for k in range(min(2, len(jobs))):
                issue_in(k)
            for k, (l, u) in enumerate(jobs):
                if k + 2 < len(jobs):
                    issue_in(k + 2)
                pieces, nk, ncols = unit_pieces(l, u)
                n = nk * ncols
                S.copy(("dve", "act")[k % 2], st16[k % 3][:, 0:n], st32[k % 3][:, 0:n])
                S.dma(wsc_d[l][u, :, 0:n], st16[k % 3][:, 0:n], queue="act")

        TILE_SEQ = ([(U_Q, 4096), (U_Q + 1, 4096), (U_XU, 4096), (U_XV, 4096)] + [(U_MRG + j, 5120) for j in range(8)]
                    + [(U_WOUT, 4096), (U_WOUT + 1, 4096)] + [(U_FIN + m, 4096) for m in range(11)]
                    + [(U_FOUT + r, 4096 if r < 5 else 2048) for r in range(6)])
        wstate = {"plan": [], "cur": 0, "issued": 0, "base": 0, "l": 0}

        def wplan_start(l, ntiles, base):
            wstate.update(plan=TILE_SEQ * ntiles, cur=0, issued=0, base=base, l=l)

        def wadvance(ahead=2):
            idx = wstate["cur"] - 1
            while wstate["issued"] < min(len(wstate["plan"]), idx + 1 + ahead):
                k = wstate["issued"]
                uu, nn = wstate["plan"][k]
                S.dma(wring[(wstate["base"] + k) % 3][:, 0:nn], wsc_d[wstate["l"]][uu, :, 0:nn])
                wstate["issued"] = k + 1

        def wload(l, u, n, ahead=2):
            idx = wstate["cur"]
            assert wstate["plan"][idx] == (u, n), (idx, u, n, wstate["plan"][idx])
            wstate["cur"] = idx + 1
            while wstate["issued"] < min(len(wstate["plan"]), idx + 1 + ahead):
                k = wstate["issued"]
                uu, nn = wstate["plan"][k]
                S.dma(wring[(wstate["base"] + k) % 3][:, 0:nn], wsc_d[wstate["l"]][uu, :, 0:nn])
                wstate["issued"] = k + 1
            return wring[(wstate["base"] + idx) % 3]

        def layer_setup(l, parts="ABCD"):
            S.tag = "setup"
            S.dma(qkg, qkg_d[l]); S.dma(bsgu, bsgu_d[l])
            S.dma(st32[0][:, 0:512].rearrange("p (a b) -> p a b", b=128), wpg_d[l], track_in=False)
            S.copy("dve", wpgb, st32[0][:, 0:512].rearrange("p (a b) -> p a b", b=128))
            S.dma(st32[0][:, 512:1024].rearrange("p (a b) -> p a b", b=128), wsT_d[l], track_in=False)
            S.copy("dve", wsTb, st32[0][:, 512:1024].rearrange("p (a b) -> p a b", b=128))
            if 'B' not in parts:
                return
            b = S.bank()
            nxt = None
            for uu in range(24):
                wa = wring[uu % 3][:, 0:UNIT].bitcast(F32)[:, 0:2048].rearrange("p (a b) -> p a b", b=256)
                S.dma(wa, wsrc(wada_d, l, 0, 8, uu * 256, uu * 256 + 256), track_in=False)
                for jj in range(2):
                    j = uu * 2 + jj
                    for kc in range(8):
                        S.mm(ps[b][:, 2 * j:2 * j + 2], wa[:, kc, jj * 128:(jj + 1) * 128], sil[:, kc, :], start=(kc == 0), stop=(kc == 7))
            S.tt("dve", mod, ps[b][:, 0:96].rearrange("p (a b) -> p a b", b=2), bcast(badaT[:, l, :], post=(2,)), ALU.add)
            S.free(b)
            if 'C' not in parts:
                return
            for g in range(2):
                S.ts("dve", scal[:, g, 0, :], mod[:, 8:16, g], 1.0, None, ALU.add)
                S.copy("dve", scal[:, g, 1, :], mod[:, 0:8, g])
                S.ts("dve", scal[:, g, 3, :], mod[:, 32:40, g], 1.0, None, ALU.add)
                S.tt("dve", scal[:, g, 2, :], lnT[:, l, 0, :], scal[:, g, 3, :], ALU.mult)
                S.tt("dve", scal[:, g, 3, :], lnT[:, l, 1, :], scal[:, g, 3, :], ALU.mult)
                S.tt("dve", scal[:, g, 3, :], scal[:, g, 3, :], mod[:, 24:32, g], ALU.add)
                if 'D' not in parts:
                    continue
                for gi, base in ((0, 16), (1, 40)):
                    for half in range(2):
                        b = S.bank()
                        for c4 in range(4):
                            c = half * 4 + c4
                            dg = (nsq, nqn)[c4 % 2][:, 0:128]
                            S.ts("dve", dg, ident_f, mod[:, base + c, g:g + 1], None, ALU.mult)
                            S.tag = "setup3"
                            S.mm(ps[b][:, c4 * 128:(c4 + 1) * 128], ones_f, dg)
                            S.tag = "setup"
                        S.copy("act", gbc[g][gi][:, half * 512:(half + 1) * 512], ps[b][:])
                        S.free(b)

        def load_x(src_d, row0, dst=None):
            dst = xt if dst is None else dst
            for s in range(4):
                S.dma(dst[:, s, :], src_d[row0 + s * 128: row0 + (s + 1) * 128, :], queue="pool")

        def make_hT(g, i_scale, i_shift, xsrc=None):
            xsrc = xt if xsrc is None else xsrc
            for c in range(8):
                b = S.bank()
                for s in range(4):
                    S.tr(ps[b][:, s * 128:(s + 1) * 128], xsrc[:, s, c * 128:(c + 1) * 128], ident_f)
                S.ts("dve", hT[:, c, :], ps[b][:], scal[:, g, i_scale, c:c + 1], scal[:, g, i_shift, c:c + 1], ALU.mult, ALU.add)
                S.free(b)

        def rms_rope(psv, nh, gain, st, out_bf, out_f32=None, tset=0):
            n = nh * 128
            t3 = (nsq, nqn, nB) if tset == 0 else tsetB
            sq = t3[0][:, 0:n]; qn = t3[1][:, 0:n]; B = t3[2][:, 0:n]
            ss = ss2[tset]
            S.act(sq, psv, AF.Square)
            S.op("dve", lambda h: h.reduce_sum(ss[:, 0:nh], sq.rearrange("p (a b) -> p a b", b=128), AX.X), [sq], [ss[:, 0:nh]])
            S.act(ss[:, 0:nh], ss[:, 0:nh], AF.Ln, bias=eps_t[:, 0:1], scale=1.0 / 128)
            S.act(ss[:, 0:nh], ss[:, 0:nh], AF.Exp, scale=-0.5)
            qn3 = qn.rearrange("p (a b) -> p a b", b=128)
            S.tt("dve", qn3, psv.rearrange("p (a b) -> p a b", b=128), bcast(ss[:, 0:nh], post=(128,)), ALU.mult)
            S.tt("dve", qn3, qn3, bcast(gain, pre=(nh,)), ALU.mult)
            if out_f32 is not None:
                S.dma(out_f32, qn, queue="pool", is_output=True, track_out=False)
            if st is None:
                S.copy("dve", out_bf, qn)
                return
            A3 = sq.rearrange("p (a b) -> p a b", b=128)
            B3 = B.rearrange("p (a b) -> p a b", b=128)
            for a in range(2):
                cs = ropeR[:, 0, st, :] if a == 0 else ropeC[:, 0, :]
                sn = ropeR[:, 1, st, :] if a == 0 else ropeC[:, 1, :]
                qa = qn3[:, :, a * 64:(a + 1) * 64].rearrange("p h (x i) -> p h x i", x=2)
                Aa = A3[:, :, a * 64:(a + 1) * 64].rearrange("p h (x i) -> p h x i", x=2)
                Ba = B3[:, :, a * 64:(a + 1) * 64].rearrange("p h (x i) -> p h x i", x=2)
                S.tt("pool", Aa, qa, bcast(cs, pre=(nh, 2)), ALU.mult)
                S.stt("dve", Ba[:, :, 0, :], qa[:, :, 1, :], -1.0, bcast(sn, pre=(nh,)), ALU.mult, ALU.mult)
                S.tt("dve", Ba[:, :, 1, :], qa[:, :, 0, :], bcast(sn, pre=(nh,)), ALU.mult)
            S.tt("dve", out_bf, sq, B, ALU.add)

        def prepass_tile(l, g, src_d, row0, key0, vch0, st0, wkv, wpin, xp_dst, nk_out=None, nv_out=None, xbuf=None, nxt=None):
            S.tag = "prepass"
            if nxt is not None:
                load_x(nxt[0], nxt[1], dst=nxt[2])
            make_hT(g, 0, 1, xsrc=xbuf)
            for s in range(4):
                b = S.bank(); b2 = S.bank()
                for kc in range(8):
                    S.mm(ps[b][:, 0:256], hT[:, kc, s * 128:(s + 1) * 128], wkv[:, kc, 0:256], start=(kc == 0), stop=(kc == 7))
                for kc in range(8):
                    S.mm(ps[b2][:, 0:256], hT[:, kc, s * 128:(s + 1) * 128], wkv[:, kc, 256:512], start=(kc == 0), stop=(kc == 7))
                if g == 0:
                    S.copy("act", Vst[:, vch0 + s, :], ps[b2][:, 0:256])
                    rms_rope(ps[b][:, 0:256], 2, qkg[:, 1, :], st0 + s, kr[:, s, :], tset=s % 2)
                else:
                    S.copy("act", vtmp, ps[b2][:, 0:256])
                    S.dma(nv_out(s), vtmp, queue="pool", is_output=True, track_out=False)
                    S.copy("dve", Vst[:, vch0 + s, :], vtmp)
                    rms_rope(ps[b][:, 0:256], 2, qkg[:, 1, :], None, kr[:, s, :], out_f32=nk_out(s), tset=s % 2)
                S.free(b); S.free(b2)
            for gg in range(4):
                b = S.bank()
                for kc in range(8):
                    S.mm(ps[b][:], wpin[:, kc, gg * 128:(gg + 1) * 128], hT[:, kc, :], start=(kc == 0), stop=(kc == 7))
                S.copy("act", xpstage[:, gg, :], ps[b][:])
                S.free(b)
                for dst, c0, c1 in xp_dst(gg):
                    S.dma(dst, xpstage[:, gg, c0:c1], queue="pool")
            for kvh in range(2):
                b = S.bank()
                for s in range(4):
                    S.tr(psb[b][:, s * 128:(s + 1) * 128], kr[:, s, kvh * 128:(kvh + 1) * 128], ident_b)
                S.copy("act", KT[:, kvh, key0:key0 + 512], psb[b][:, 0:512])
                S.free(b)

        def load_cache(l):
            S.tag = "cache"
            for (src, isk) in ((ck_d, True), (cv_d, False)):
                S.dma(cstage, src[l].rearrange("(c p) n -> p c n", p=128), queue="pool")
                if isk:
                    S.copy("dve", cstb, cstage)
                    for kvh in range(2):
                        b = S.bank()
                        for s in range(4):
                            S.tr(psb[b][:, s * 128:(s + 1) * 128], cstb[:, s, kvh * 128:(kvh + 1) * 128], ident_b)
                        S.copy("act", KT[:, kvh, 0:512], psb[b][:, 0:512])
                        S.free(b)
                else:
                    S.copy("dve", Vst[:, 0:4, :], cstage)

        def attention(h, kvh, qc0, nq, chunks):
            ob = S.bank(); db = S.bank()
            n = len(chunks)
            LA = 2
            sb = {}
            for i in range(n + LA):
                if i < n:
                    k0, vc = chunks[i]
                    b = S.bank(); sb[i] = b
                    S.mm(ps[b][:, 0:nq], KT[:, kvh, k0:k0 + 128], qT[:, h, qc0:qc0 + nq])
                    S.act(Pr[i % 4][:, 0:nq], ps[b][:, 0:nq], AF.Exp, scale=ATT_SCALE)
                    S.free(b)
                j = i - LA
                if j >= 0:
                    k0, vc = chunks[j]
                    S.mm(ps[ob][:, 0:nq], Vst[:, vc, kvh * 128:(kvh + 1) * 128], Pr[j % 4][:, 0:nq], start=(j == 0), stop=(j == n - 1))
                    S.mm(ps[db][:, 0:nq], ones_b, Pr[j % 4][:, 0:nq], start=(j == 0), stop=(j == n - 1))
            S.op("dve", lambda hh: hh.reciprocal(rden[:, 0:nq], ps[db][:, 0:nq]), [ps[db][:, 0:nq]], [rden[:, 0:nq]])
            S.tt("dve", attnT[:, h, qc0:qc0 + nq], ps[ob][:, 0:nq], rden[:, 0:nq], ALU.mult)
            S.free(ob); S.free(db)

        def attention2(h, kvh, qc0, nq, chunks, ob, db, srot):
            n = len(chunks)
            LA = 2
            for i in range(n + LA):
                if i < n:
                    k0, vc = chunks[i]
                    b = srot[i % 3]
                    S.mm(ps[b][:, 0:nq], KT[:, kvh, k0:k0 + 128], qT[:, h, qc0:qc0 + nq])
                    S.act(Pr[i % 4][:, 0:nq], ps[b][:, 0:nq], AF.Exp, scale=ATT_SCALE)
                j = i - LA
                if j >= 0:
                    k0, vc = chunks[j]
                    S.mm(ps[ob][:, 0:nq], Vst[:, vc, kvh * 128:(kvh + 1) * 128], Pr[j % 4][:, 0:nq], start=(j == 0), stop=(j == n - 1))
                    S.mm(ps[db][:, 0:nq], ones_b, Pr[j % 4][:, 0:nq], start=(j == 0), stop=(j == n - 1))
            def tail():
                S.op("dve", lambda hh: hh.reciprocal(rden[:, 0:nq], ps[db][:, 0:nq]), [ps[db][:, 0:nq]], [rden[:, 0:nq]])
                S.tt("dve", attnT[:, h, qc0:qc0 + nq], ps[ob][:, 0:nq], rden[:, 0:nq], ALU.mult)
            return tail

        def pool_segment(o, L, left_edge, right_edge, oc):
            W = L + 16
            for g in range(4):
                x = xpb[:, g, o:o + W]
                S.tt("pool", sA[:, 1:W], x[:, 0:W - 1], x[:, 1:W], ALU.add)
                if g == 0:
                    win = sA[:, 8:8 + L]
                elif g == 1:
                    S.tt("pool", sB[:, 8:8 + L], sA[:, 7:7 + L], sA[:, 9:9 + L], ALU.add)
                    win = sB[:, 8:8 + L]
                else:
                    S.tt("pool", sB[:, 3:W], sA[:, 1:W - 2], sA[:, 3:W], ALU.add)
                    if g == 2:
                        S.tt("pool", sC[:, 8:8 + L], sB[:, 7:7 + L], sB[:, 11:11 + L], ALU.add)
                        win = sC[:, 8:8 + L]
                    else:
                        S.tt("pool", sC[:, 7:W], sB[:, 3:W - 4], sB[:, 7:W], ALU.add)
                        S.tt("pool", sA[:, 8:8 + L], sC[:, 7:7 + L], sC[:, 15:15 + L], ALU.add)
                        win = sA[:, 8:8 + L]
                w = POOLW[g]
                S.stt("dve", pooledT[:, g, oc:oc + L], win, 1.0 / w, x[:, 8:8 + L], ALU.mult, ALU.subtract)
                hw = w // 2
                if left_edge:
                    for t in range(hw):
                        cnt = t + hw
                        S.stt("dve", pooledT[:, g, oc + t:oc + t + 1], win[:, t:t + 1], 1.0 / cnt, x[:, 8 + t:9 + t], ALU.mult, ALU.subtract)
                if right_edge:
                    for t in range(L - hw + 1, L):
                        cnt = L - t + hw
                        S.stt("dve", pooledT[:, g, oc + t:oc + t + 1], win[:, t:t + 1], 1.0 / cnt, x[:, 8 + t:9 + t], ALU.mult, ALU.subtract)

        def make_tile(l, g, src_d, row0, dst_d, st0, att_plan, xp_loads, pool_segs, is_out, nxt=None):
            sfx = "#L%dT%d" % (l, row0 // 512 if g == 0 else 8)

            def xres_load():
                load_x(src_d, row0)

            def prefetch_next():
                if nxt is not None:
                    load_x(nxt[0], nxt[1], dst=xnext)
            bctx = {"get": S.bank, "put": S.free}

            def xp_load():
                for dst, src in xp_loads:
                    S.dma(dst, src, queue="pool")
            wq = [None, None]

            def q_block(half, s, tset=0):
                S.tag = "q" + sfx
                if wq[half] is None:
                    wq[half] = wload(l, U_Q + half, 4096)[:, 0:4096].rearrange("p (a b) -> p a b", b=512)
                b = bctx["get"]()
                for kc in range(8):
                    S.mm(ps[b][:], hT[:, kc, s * 128:(s + 1) * 128], wq[half][:, kc, :], start=(kc == 0), stop=(kc == 7))
                rms_rope(ps[b][:], 4, qkg[:, 0, :], (st0 + s) if g == 0 else None, qr[:, s, half * 512:(half + 1) * 512], tset=tset)
                bctx["put"](b)

            def q_transposes(h0, h1):
                S.tag = "qT" + sfx
                for h in range(h0, h1):
                    b = bctx["get"]()
                    for s in range(4):
                        S.tr(psb[b][:, s * 128:(s + 1) * 128], qr[:, s, h * 128:(h + 1) * 128], ident_b)
                    S.copy("dve", qT[:, h, :], psb[b][:, 0:512])
                    bctx["put"](b)

            def sgu_proj(early=False):
                S.tag = "sgu" + sfx
                wxu = wload(l, U_XU, 4096, ahead=1 if early else 2)[:, 0:4096].rearrange("p (a b) -> p a b", b=512)
                for gg in range(4):
                    b = bctx["get"]()
                    for kc in range(8):
                        S.mm(ps[b][:], wxu[:, kc, gg * 128:(gg + 1) * 128], hT[:, kc, :], start=(kc == 0), stop=(kc == 7))
                    S.act(xuT[:, gg, :], ps[b][:], AF.Gelu)
                    bctx["put"](b)
                wxv = wload(l, U_XV, 4096, ahead=0 if early else 2)[:, 0:4096].rearrange("p (a b) -> p a b", b=512)
                for s in range(4):
                    b = bctx["get"]()
                    for kc in range(8):
                        S.mm(ps[b][:], hT[:, kc, s * 128:(s + 1) * 128], wxv[:, kc, :], start=(kc == 0), stop=(kc == 7))
                    S.act(xvt4[s], ps[b][:], AF.Gelu)
                    bctx["put"](b)

            def sgu_mix():
                S.tag = "sgu" + sfx
                for s in range(4):
                    xvt = xvt4[s]
                    S.op("dve", lambda hh, xvt=xvt: hh.bn_stats(bst[:, 0:6], xvt), [xvt], [bst[:, 0:6]])
                    S.op("dve", lambda hh: hh.bn_aggr(mv, bst[:, 0:6]), [bst[:, 0:6]], [mv])
                    S.act(rstd, mv[:, 1:2], AF.Ln, bias=eps_t[:, 0:1], scale=1.0)
                    S.act(rstd, rstd, AF.Exp, scale=-0.5)
                    S.ts("dve", vr[:, s, :], xvt, mv[:, 0:1], rstd, ALU.subtract, ALU.mult)
                for gg in range(4):
                    b = bctx["get"]()
                    for s in range(4):
                        S.mm(ps[b][:, s * 128:(s + 1) * 128], vr[:, s, gg * 128:(gg + 1) * 128], wsTb[:, gg, :])
                    S.tt("dve", stmp.rearrange("p (a b) -> p a b", b=128), ps[b][:].rearrange("p (a b) -> p a b", b=128),
                         bcast(bsgu[:, gg, :], pre=(4,)), ALU.add)
                    bctx["put"](b)
                    S.tt("dve", sguT[:, gg, :], stmp, xuT[:, gg, :], ALU.mult)

            def pool_stage():
                S.tag = "pool" + sfx
                for (o, L, le, re, oc) in pool_segs:
                    pool_segment(o, L, le, re, oc)
                for gg in range(4):
                    b = bctx["get"]()
                    S.mm(ps[b][:], wpgb[:, gg, :], pooledT[:, gg, :])
                    S.ts("dve", poolT[:, gg, :], ps[b][:], pscT[:, l, gg:gg + 1], None, ALU.mult)
                    bctx["put"](b)

            def _body(next_front, pending):
                pend = list(pending) if pending else [lambda: None, lambda: None]
                if g != 0:
                    pend[0](); pend[1]()
                    S.tag = "attn" + sfx
                    for (h, kvh, qc0, nq, chunks) in att_plan:
                        attention(h, kvh, qc0, nq, chunks)
                    xres_load()
                    sgu_mix(); xp_load(); pool_stage()
                    prefetch_next()
                else:
                    bs = [S.bank() for _ in range(8)]
                    misc = bs[7]
                    bctx["get"] = lambda: misc
                    bctx["put"] = lambda b: None
                    work = {0: [lambda: q_block(1, 1), pend[0]], 1: [lambda: q_block(1, 2), pend[1], xres_load], 2: [lambda: q_block(1, 3), wadvance],
                            3: [lambda: q_transposes(4, 8)], 4: [sgu_mix, xp_load], 5: [pool_stage, prefetch_next]}
                    for i, (h, kvh, qc0, nq, chunks) in enumerate(att_plan):
                        S.tag = "attn" + sfx
                        tail = attention2(h, kvh, qc0, nq, chunks, bs[2 * (i % 2)], bs[2 * (i % 2) + 1], bs[4:7])
                        for w in work.get(i, []):
                            w()
                        S.tag = "attn" + sfx
                        tail()
                    for b in bs:
                        S.free(b)
                    bctx["get"] = S.bank
                    bctx["put"] = S.free
                S.tag = "merge" + sfx
                for j in range(8):
                    wm = wload(l, U_MRG + j, 5120)[:, 0:5120].rearrange("p (a b) -> p a b", b=128)
                    gb = [S.bank() for _ in range(3)]
                    for i3 in range(3):
                        for kc in range(8):
                            S.mm(ps[gb[i3]][:], wm[:, i3 * 8 + kc, :], hT[:, kc, :], start=(kc == 0), stop=(kc == 7))
                        S.act(sig[i3], ps[gb[i3]][:], AF.Sigmoid)
                        S.free(gb[i3])
                    ob3 = [S.bank() for _ in range(3)]
                    for kc in range(8):
                        S.mm(ps[ob3[0]][:], wm[:, 24 + kc, :], attnT[:, kc, :], start=(kc == 0), stop=(kc == 7))
                    for kc in range(4):
                        S.mm(ps[ob3[1]][:], wm[:, 32 + kc, :], poolT[:, kc, :], start=(kc == 0), stop=(kc == 3))
                    for kc in range(4):
                        S.mm(ps[ob3[2]][:], wm[:, 36 + kc, :], sguT[:, kc, :], start=(kc == 0), stop=(kc == 3))
                    S.tt("dve", mtm, sig[0], ps[ob3[0]][:], ALU.mult)
                    S.tt("dve", mtt, sig[1], ps[ob3[1]][:], ALU.mult)
                    S.tt("pool", mtm, mtm, mtt, ALU.add)
                    S.tt("dve", mtt, sig[2], ps[ob3[2]][:], ALU.mult)
                    S.tt("dve", mergedT[:, j, :], mtm, mtt, ALU.add)
                    for b in ob3:
                        S.free(b)
                S.tag = "wout" + sfx
                S.dma(bcr[0], lnbc_d[l, 0], queue="pool"); S.dma(bcr[1], lnbc_d[l, 1], queue="pool")
                wo2 = [wload(l, U_WOUT + half, 4096, ahead=2 - half)[:, 0:4096].rearrange("p (a b) -> p a b", b=512) for half in range(2)]
                wt2 = [wtmp, sig[1]]
                for s in range(4):
                    S.tag = "wout" + sfx
                    for half in range(2):
                        b = S.bank()
                        for kc in range(8):
                            S.mm(ps[b][:], mergedT[:, kc, s * 128:(s + 1) * 128], wo2[half][:, kc, :], start=(kc == 0), stop=(kc == 7))
                        S.tt("dve", wt2[half], ps[b][:], gbc[g][0][:, half * 512:(half + 1) * 512], ALU.mult)
                        S.free(b)
                        xs_ = xt[:, s, half * 512:(half + 1) * 512]
                        S.stt("dve", xs_, xs_, ALPHA, wt2[half], ALU.mult, ALU.add)
                    layer_norm_sub(s)
                S.tag = "hT2" + sfx
                make_hT(g, 2, 3)
                for s in range(4):
                    S.tt("pool", xt[:, s, :], xt[:, s, :], bcr[0], ALU.mult)
                    S.tt("pool", xt[:, s, :], xt[:, s, :], bcr[1], ALU.add)
                S.tag = "ffnin" + sfx
                for m in range(11):
                    wf = wload(l, U_FIN + m, 4096)[:, 0:4096].rearrange("p (a b) -> p a b", b=512)
                    for jj in range(2):
                        ba = S.bank(); bb = S.bank()
                        for kc in range(8):
                            S.mm(ps[ba][:], wf[:, kc, jj * 128:(jj + 1) * 128], hT[:, kc, :], start=(kc == 0), stop=(kc == 7))
                        for kc in range(8):
                            S.mm(ps[bb][:], wf[:, kc, 256 + jj * 128:256 + (jj + 1) * 128], hT[:, kc, :], start=(kc == 0), stop=(kc == 7))
                        S.act(ftmp, ps[ba][:], AF.Silu)
                        S.free(ba)
                        S.tt("dve", uT[:, 2 * m + jj, :], ftmp, ps[bb][:], ALU.mult)
                        S.free(bb)
                S.tag = "ffnout" + sfx
                S.dma(bcr[0], lnbc_d[l, 2], queue="pool"); S.dma(bcr[1], lnbc_d[l, 3], queue="pool")
                acc = [[S.bank() for _ in range(2)] for _ in range(4)]
                for r in range(6):
                    nkk = 4 if r < 5 else 2
                    wo = wload(l, U_FOUT + r, nkk * 1024)[:, 0:nkk * 1024].rearrange("p (a b) -> p a b", b=1024)
                    for kk in range(nkk):
                        kc = 4 * r + kk
                        for s in range(4):
                            for half in range(2):
                                S.mm(ps[acc[s][half]][:], uT[:, kc, s * 128:(s + 1) * 128], wo[:, kk, half * 512:(half + 1) * 512],
                                     start=(kc == 0), stop=(kc == 21))
                for s in range(4):
                    for half in range(2):
                        S.tag = "ffnout" + sfx
                        S.tt("dve", ftmp, ps[acc[s][half]][:], gbc[g][1][:, half * 512:(half + 1) * 512], ALU.mult)
                        S.free(acc[s][half])
                        xs_ = xt[:, s, half * 512:(half + 1) * 512]
                        S.stt("dve", xs_, xs_, ALPHA, ftmp, ALU.mult, ALU.add)
                    if s == 0 and next_front is not None:
                        next_front[1]()
                def ln2_stats():
                    S.tag = "ln2" + sfx
                    for s in range(4):
                        S.op("dve", lambda hh, s=s: hh.bn_stats(bst4[:, s, 0:6], xt[:, s, 0:512]), [xt[:, s, 0:512]], [bst4[:, s, 0:6]])
                        S.op("dve", lambda hh, s=s: hh.bn_stats(bst4[:, s, 6:12], xt[:, s, 512:1024]), [xt[:, s, 512:1024]], [bst4[:, s, 6:12]])
                        S.op("dve", lambda hh, s=s: hh.bn_aggr(mv4[:, s, :], bst4[:, s, :]), [bst4[:, s, :]], [mv4[:, s, :]])

                def ln2_finish():
                    S.tag = "ln2" + sfx
                    S.act(rstd4, mv4[:, :, 1], AF.Ln, bias=eps_t[:, 0:1], scale=1.0)
                    S.act(rstd4, rstd4, AF.Exp, scale=-0.5)
                    S.stt("dve", nmr4, mv4[:, :, 0], -1.0, rstd4, ALU.mult, ALU.mult)
                    for s in range(4):
                        S.ts("dve", xt[:, s, :], xt[:, s, :], rstd4[:, s:s + 1], nmr4[:, s:s + 1], ALU.mult, ALU.add)
                        S.tt("pool", xt[:, s, :], xt[:, s, :], bcr[0], ALU.mult)
                        S.tt("pool", xt[:, s, :], xt[:, s, :], bcr[1], ALU.add)
                        S.dma(dst_d[row0 + s * 128: row0 + (s + 1) * 128, :], xt[:, s, :], queue="pool", is_output=is_out, track_out=not is_out)

                if next_front is not None:
                    next_front[2]()
                    return (ln2_stats, ln2_finish)
                ln2_stats(); ln2_finish()
                return None


            def front_a():
                S.tag = "hT1" + sfx
                make_hT(g, 0, 1, xsrc=xnext)

            def front_b():
                if g != 0:
                    for half in range(2):
                        for s in range(4):
                            q_block(half, s, tset=s % 2)
                    sgu_proj()
                    q_transposes(0, 8)
                else:
                    for s in range(4):
                        q_block(0, s, tset=s % 2)
                    q_block(1, 0, tset=0)
                    sgu_proj(early=True)
                    q_transposes(0, 4)

            def front():
                front_a(); front_b()

            def body(next_front=None, pending=None):
                return _body(next_front, pending)
            return (front, front_a, front_b), body

        def layer_norm_tile():
            for s in range(4):
                layer_norm_sub(s)

        def layer_norm_sub(s):
            if True:
                S.op("dve", lambda hh, s=s: hh.bn_stats(bst[:, 0:6], xt[:, s, 0:512]), [xt[:, s, 0:512]], [bst[:, 0:6]])
                S.op("dve", lambda hh, s=s: hh.bn_stats(bst[:, 6:12], xt[:, s, 512:1024]), [xt[:, s, 512:1024]], [bst[:, 6:12]])
                S.op("dve", lambda hh: hh.bn_aggr(mv, bst), [bst], [mv])
                S.act(rstd, mv[:, 1:2], AF.Ln, bias=eps_t[:, 0:1], scale=1.0)
                S.act(rstd, rstd, AF.Exp, scale=-0.5)
                S.stt("dve", nmr, mv[:, 0:1], -1.0, rstd, ALU.mult, ALU.mult)
                S.ts("dve", xt[:, s, :], xt[:, s, :], rstd, nmr, ALU.mult, ALU.add)

        if "prep" in phases:
            prep_all([0, 1])
        S.barrier()
        for l in range(2):
            if ("l%d" % l) not in phases:
                continue
            if (dbg or {}).get('setup', True):
                layer_setup(l, (dbg or {}).get('parts', 'ABCD'))
            src_s, dst_s = (xs_d, x1s_d) if l == 0 else (x1s_d, ys_d)
            src_p, dst_p = (xp_d, x1p_d) if l == 0 else (x1p_d, yp_d)
            if dbg_l0_out and l == 0:
                dst_s, dst_p = ys_d, yp_d
            last = (l == 1) or dbg_l0_out
            wkv = wring[0][:, 0:4096].rearrange("p (a b) -> p a b", b=512)
            wpin = wring[1][:, 0:4096].rearrange("p (a b) -> p a b", b=512)
            S.dma(wring[0][:, 0:4096], wsc_d[l][U_KV, :, 0:4096])
            S.dma(wring[1][:, 0:4096], wsc_d[l][U_PIN, :, 0:4096])
            ntl = (dbg or {}).get("n_main_s", NSAMP_T) + (1 if (dbg or {}).get("main_p", True) else 0)
            wplan_start(l, ntl, 2)
            dbg_ = dbg or {}
            if dbg_.get("cache", True):
                load_cache(l)
            nps = dbg_.get("n_pre_s", NSAMP_T)
            do_pp = dbg_.get("pre_p", True)
            xb = [xt, xnext]
            pre = [(src_s, T * 512) for T in range(nps)] + ([(src_p, 0)] if do_pp else [])
            if pre:
                load_x(pre[0][0], pre[0][1], dst=xb[0])
            for k, (sd, r0) in enumerate(pre):
                nx = (pre[k + 1][0], pre[k + 1][1], xb[(k + 1) % 2]) if k + 1 < len(pre) else None
                if sd is src_s:
                    T = k
                    prepass_tile(l, 0, src_s, T * 512, 512 + T * 512, 4 + T * 4, T * 4, wkv, wpin,
                                 lambda gg, T=T: [(xps_d[gg, :, 8 + T * 512: 8 + T * 512 + 512], 0, 512)], xbuf=xb[k % 2], nxt=nx)
                else:
                    prepass_tile(l, 1, src_p, 0, KOFF_P, 36, None, wkv, wpin,
                                 lambda gg: [(xpp_d[gg, :, 0, 8:264], 0, 256), (xpp_d[gg, :, 1, 8:264], 256, 512)],
                                 nk_out=lambda s, l=l: nk_d[s // 2, l, (s % 2) * 128:(s % 2) * 128 + 128, :],
                                 nv_out=lambda s, l=l: nv_d[s // 2, l, (s % 2) * 128:(s % 2) * 128 + 128, :], xbuf=xb[k % 2], nxt=nx)
            nms = dbg_.get("n_main_s", NSAMP_T)
            do_p = dbg_.get("main_p", True)
            if nms > 0:
                load_x(src_s, 0, dst=xnext)
            elif do_p:
                load_x(src_p, 0, dst=xnext)
            tiles = []
            for T in range(nms):
                chunks = [(kc * 128, kc) for kc in range(36)]
                plan = [(h, h // 4, 0, 512, chunks) for h in range(8)]
                xp_loads = [(xpb[:, gg, 0:528], xps_d[gg, :, T * 512: T * 512 + 528]) for gg in range(4)]
                segs = [(0, 512, T == 0, T == NSAMP_T - 1, 0)]
                nxt = (src_s, (T + 1) * 512) if T + 1 < nms else ((src_p, 0) if do_p else None)
                tiles.append(make_tile(l, 0, src_s, T * 512, dst_s, T * 4, plan, xp_loads, segs, last, nxt=nxt))
            if do_p:
                plan = []
                for sq_ in range(2):
                    chunks = [(KOFF_P + sq_ * 256 + i * 128, 36 + sq_ * 2 + i) for i in range(2)]
                    for h in range(8):
                        plan.append((h, h // 4, sq_ * 256, 256, chunks))
                xp_loads = [(xpb[:, gg, :].rearrange("p (a b) -> p a b", b=272), xpp_d[gg]) for gg in range(4)]
                segs = [(0, 256, True, True, 0), (272, 256, True, True, 256)]
                tiles.append(make_tile(l, 1, src_p, 0, dst_p, None, plan, xp_loads, segs, last))
            if tiles:
                tiles[0][0][0]()
            pending = None
            for ti, (fr, bd) in enumerate(tiles):
                pending = bd(tiles[ti + 1][0] if ti + 1 < len(tiles) else None, pending)
        S.emit(es)
    return nc, S


def _rope_tables():
    inv = (np.float32(10000.0) ** (-np.arange(32, dtype=np.float32) / np.float32(32))).astype(np.float32)
    p = np.arange(128)
    ropeR = np.zeros((128, 2, 32, 32), np.float32)
    for st in range(32):
        row = (2 * st + p // 64).astype(np.float32)
        ang = (row[:, None] * inv[None, :]).astype(np.float32)
        ropeR[:, 0, st, :] = np.cos(ang)
        ropeR[:, 1, st, :] = np.sin(ang)
    col = (p % 64).astype(np.float32)
    ang = (col[:, None] * inv[None, :]).astype(np.float32)
    ropeC = np.stack([np.cos(ang), np.sin(ang)], axis=1).astype(np.float32)
    return ropeR, ropeC


_CACHE = {}


def make_in_maps(x_prompt, x_sample, cache_k, cache_v, c, c_ctx, w_ada, b_ada, w_in, q_norm_g, k_norm_g,
                 w_pool_g, pool_scale, w_sgu, b_sgu, w_attn_o, w_pool_o, w_sgu_o, w_out, ln1_g, ln1_b,
                 w_ffn_in, w_ffn_out, ln2_g, ln2_b):
    f = lambda a: np.ascontiguousarray(np.asarray(a, dtype=np.float32))
    x_prompt, x_sample, cache_k, cache_v, c, c_ctx = map(f, (x_prompt, x_sample, cache_k, cache_v, c, c_ctx))
    ropeR, ropeC = _rope_tables()
    bc = lambda v, n=128: np.ascontiguousarray(np.broadcast_to(v, (n,) + v.shape))
    qkg = np.stack([f(q_norm_g), f(k_norm_g)], axis=1)
    qkg = np.ascontiguousarray(np.broadcast_to(qkg[:, None], (2, 128, 2, 128)))
    ln = np.stack([f(ln1_g), f(ln1_b), f(ln2_g), f(ln2_b)], axis=1)
    lnbc = np.ascontiguousarray(np.broadcast_to(ln[:, :, None, :], (2, 4, 128, 1024)))
    lnT = np.ascontiguousarray(ln[:, 0:2].reshape(2, 2, 8, 128).transpose(3, 0, 1, 2))
    shared = dict(
        w_ada=f(w_ada), b_adaT=np.ascontiguousarray(f(b_ada).reshape(2, 48, 128).transpose(2, 0, 1)),
        w_in=f(w_in), w_attn_o=f(w_attn_o), w_pool_o=f(w_pool_o), w_sgu_o=f(w_sgu_o), w_out=f(w_out),
        w_ffn_in=f(w_ffn_in), w_ffn_out=f(w_ffn_out),
        w_pg=np.ascontiguousarray(f(w_pool_g).transpose(0, 2, 1, 3)),
        w_sT=np.ascontiguousarray(f(w_sgu).transpose(0, 3, 1, 2)),
        bsgu=np.ascontiguousarray(np.broadcast_to(f(b_sgu)[:, None], (2, 128, 4, 128))),
        pscT=np.ascontiguousarray(f(pool_scale).reshape(2, 4, 128).transpose(2, 0, 1)),
        qkg=qkg, lnbc=lnbc, lnT=lnT, ropeR=ropeR, ropeC=ropeC, idf=np.eye(128, dtype=np.float32),
    )
    in_maps = []
    for i in range(8):
        m = dict(shared)
        m["xs"] = x_sample[i]
        m["xp"] = np.ascontiguousarray(x_prompt[2 * i:2 * i + 2].reshape(512, 1024))
        m["ck"] = np.ascontiguousarray(cache_k[i].reshape(2, 512, 256))
        m["cv"] = np.ascontiguousarray(cache_v[i].reshape(2, 512, 256))
        cT = np.stack([c[i].reshape(8, 128).T, c_ctx.reshape(8, 128).T], axis=2)
        m["cT"] = np.ascontiguousarray(cT.astype(np.float32))
        in_maps.append(m)
    return in_maps


def kernel(**inputs):
    if "nc" not in _CACHE:
        _CACHE["nc"] = build_program()[0]
    nc = _CACHE["nc"]
    in_maps = make_in_maps(**inputs)
    res = run_bass_kernel_spmd(nc, in_maps, core_ids=list(range(8)))
    y_p = np.zeros((16, 256, 1024), np.float32)
    y_s = np.zeros((8, 4096, 1024), np.float32)
    nk = np.zeros((16, 2, 256, 2, 128), np.float32)
    nv = np.zeros((16, 2, 256, 2, 128), np.float32)
    for i in range(8):
        r = res.results[i]
        y_s[i] = r["ys"]
        y_p[2 * i:2 * i + 2] = r["yp"].reshape(2, 256, 1024)
        nk[2 * i:2 * i + 2] = r["nk"].reshape(2, 2, 256, 2, 128)
        nv[2 * i:2 * i + 2] = r["nv"].reshape(2, 2, 256, 2, 128)
    return (y_p, y_s, nk, nv)
```
